# Optimizing a Trainium2 kernel written in Bass

```python
import math
import jax, jax.numpy as jnp
from jax import lax
import numpy as np

D_MODEL = 1024
BATCH = 16
SEQ = 4096
DEPTH = 1
DEC_BATCH = 2
DEC_SEQ = 8192
PAST_LEN = 128

A_HEADS = 8
A_QK_DIM = 64
A_V_DIM = 2 * A_QK_DIM
A_QK_WIDTH = 2 * A_HEADS * A_QK_DIM
A_WIDTH = A_HEADS * A_V_DIM
B_HEADS = 8
B_KV_HEADS = 2
B_GROUP = B_HEADS // B_KV_HEADS
B_HEAD_DIM = 128
B_WIDTH = B_HEADS * B_HEAD_DIM
B_KV_WIDTH = B_KV_HEADS * B_HEAD_DIM
ROPE_AXIS_DIM = B_HEAD_DIM // 2
ROPE_THETA = 10000.0
GRID_W = 64
REL_BUCKETS = 32
REL_MAX_DIST = 128
Q_BLOCK = 128
EPS = 1e-6
IN_SIZES = (A_QK_WIDTH, A_QK_WIDTH, A_WIDTH, A_WIDTH, B_WIDTH, B_KV_WIDTH, B_KV_WIDTH, B_WIDTH, D_MODEL, D_MODEL)
IN_WIDTH = 2 * A_QK_WIDTH + 2 * A_WIDTH + 2 * B_WIDTH + 2 * B_KV_WIDTH + 2 * D_MODEL

kernel_name = "hybrid_gated_diffattn_gqa_axialrope_encoder"


def _rms_norm(x, g):
    x32 = x.astype(jnp.float32)
    y = x32 * lax.rsqrt(jnp.mean(x32 * x32, axis=-1, keepdims=True) + EPS)
    return (y * g.astype(jnp.float32)).astype(x.dtype)


def _to_blocks(q):
    shp = q.shape
    q = q.reshape(shp[:-2] + (shp[-2] // Q_BLOCK, Q_BLOCK, shp[-1]))
    return jnp.moveaxis(q, -3, 0)


def _from_blocks(o):
    o = jnp.moveaxis(o, 0, -3)
    shp = o.shape
    return o.reshape(shp[:-3] + (shp[-3] * shp[-2], shp[-1]))


def _rel_bucket(rel):
    half = REL_BUCKETS // 2
    max_exact = half // 2
    ret = (rel > 0).astype(jnp.int32) * half
    n = jnp.abs(rel)
    nf = jnp.maximum(n, 1).astype(jnp.float32)
    large = max_exact + (jnp.log(nf / max_exact) / math.log(REL_MAX_DIST / max_exact)
                         * (half - max_exact)).astype(jnp.int32)
    large = jnp.minimum(large, half - 1)
    return ret + jnp.where(n < max_exact, n, large)


def _diff_attention(q1, q2, k1, k2, v, rel_bias, lam):
    S = q1.shape[2]
    nb = S // Q_BLOCK
    scale = A_QK_DIM ** -0.5
    kpos = jnp.arange(S, dtype=jnp.int32)

    def blk(args):
        i, qa, qb = args
        qpos = i * Q_BLOCK + jnp.arange(Q_BLOCK, dtype=jnp.int32)
        bucket = _rel_bucket(kpos[None, :] - qpos[:, None])
        bias = jnp.transpose(rel_bias[bucket], (2, 0, 1)).astype(jnp.float32)
        s1 = jnp.einsum('bhqd,bhkd->bhqk', qa, k1).astype(jnp.float32) * scale + bias
        s2 = jnp.einsum('bhqd,bhkd->bhqk', qb, k2).astype(jnp.float32) * scale + bias
        p = jax.nn.softmax(s1, axis=-1) - lam * jax.nn.softmax(s2, axis=-1)
        return jnp.einsum('bhqk,bhkd->bhqd', p.astype(v.dtype), v)

    out = lax.map(blk, (jnp.arange(nb, dtype=jnp.int32), _to_blocks(q1), _to_blocks(q2)))
    return _from_blocks(out)


def _gqa_attention(q, k, v):
    scale = B_HEAD_DIM ** -0.5

    def blk(qb):
        s = jnp.einsum('bngqd,bnkd->bngqk', qb, k).astype(jnp.float32) * scale
        p = jax.nn.softmax(s, axis=-1)
        return jnp.einsum('bngqk,bnkd->bngqd', p.astype(v.dtype), v)

    return _from_blocks(lax.map(blk, _to_blocks(q)))


def _axial_angles(S):
    rows = S // GRID_W
    row_idx = jnp.repeat(jnp.arange(rows, dtype=jnp.float32), GRID_W)
    col_idx = jnp.tile(jnp.arange(GRID_W, dtype=jnp.float32), rows)
    inv_freq = ROPE_THETA ** (-jnp.arange(0, ROPE_AXIS_DIM, 2, dtype=jnp.float32) / ROPE_AXIS_DIM)
    ang_r = row_idx[:, None] * inv_freq[None, :]
    ang_c = col_idx[:, None] * inv_freq[None, :]
    return jnp.cos(ang_r), jnp.sin(ang_r), jnp.cos(ang_c), jnp.sin(ang_c)


def _rot(u, c, s):
    h = u.shape[-1] // 2
    u1, u2 = u[..., :h], u[..., h:]
    return jnp.concatenate([u1 * c - u2 * s, u1 * s + u2 * c], axis=-1)


def _apply_axial_rope(x, angles):
    cr, sr, cc, sc = [a[:, None, :] for a in angles]
    x32 = x.astype(jnp.float32)
    out = jnp.concatenate([_rot(x32[..., :ROPE_AXIS_DIM], cr, sr),
                           _rot(x32[..., ROPE_AXIS_DIM:], cc, sc)], axis=-1)
    return out.astype(x.dtype)


def _layer(x, l, g_norm, w_in, lambda_q1, lambda_k1, lambda_q2, lambda_k2, subln_w,
           q_norm_b, k_norm_b, w_proj_a, w_proj_b, w_out, rel_bias):
    B, S, _ = x.shape
    xn = _rms_norm(x, g_norm)
    proj = xn @ w_in
    offs = [int(o) for o in np.cumsum(IN_SIZES)[:-1]]
    qa, ka, va, za, qb, kb, vb, zb, ga, gb = jnp.split(proj, offs, axis=-1)

    lambda_init = 0.8 - 0.6 * math.exp(-0.3 * l)
    qa = jnp.transpose(qa.reshape(B, S, A_HEADS, 2, A_QK_DIM), (3, 0, 2, 1, 4))
    ka = jnp.transpose(ka.reshape(B, S, A_HEADS, 2, A_QK_DIM), (3, 0, 2, 1, 4))
    va = jnp.transpose(va.reshape(B, S, A_HEADS, A_V_DIM), (0, 2, 1, 3))
    lam = (jnp.exp(jnp.sum(lambda_q1.astype(jnp.float32) * lambda_k1.astype(jnp.float32)))
           - jnp.exp(jnp.sum(lambda_q2.astype(jnp.float32) * lambda_k2.astype(jnp.float32)))
           + lambda_init)
    oa = _diff_attention(qa[0], qa[1], ka[0], ka[1], va, rel_bias, lam)
    oa = _rms_norm(oa, subln_w) * (1.0 - lambda_init)
    oa = jnp.transpose(oa, (0, 2, 1, 3)).reshape(B, S, A_WIDTH)
    ya = (oa * jax.nn.silu(za)) @ w_proj_a

    angles = _axial_angles(S)
    qb = _apply_axial_rope(_rms_norm(qb.reshape(B, S, B_HEADS, B_HEAD_DIM), q_norm_b), angles)
    kb = _apply_axial_rope(_rms_norm(kb.reshape(B, S, B_KV_HEADS, B_HEAD_DIM), k_norm_b), angles)
    qb = jnp.transpose(qb.reshape(B, S, B_KV_HEADS, B_GROUP, B_HEAD_DIM), (0, 2, 3, 1, 4))
    kb = jnp.transpose(kb, (0, 2, 1, 3))
    vb = jnp.transpose(vb.reshape(B, S, B_KV_HEADS, B_HEAD_DIM), (0, 2, 1, 3))
    ob = _gqa_attention(qb, kb, vb)
    ob = jnp.transpose(ob, (0, 3, 1, 2, 4)).reshape(B, S, B_WIDTH)
    yb = (ob * jax.nn.silu(zb)) @ w_proj_b

    merged = jax.nn.sigmoid(ga) * ya + jax.nn.sigmoid(gb) * yb
    return x + merged @ w_out


def _trunk(x, g_norm, w_in, lambda_q1, lambda_k1, lambda_q2, lambda_k2, subln_w,
           q_norm_b, k_norm_b, w_proj_a, w_proj_b, w_out, rel_bias, g_final):
    h = x
    for l in range(DEPTH):
        h = _layer(h, l, g_norm[l], w_in[l], lambda_q1[l], lambda_k1[l], lambda_q2[l], lambda_k2[l],
                   subln_w[l], q_norm_b[l], k_norm_b[l], w_proj_a[l], w_proj_b[l], w_out[l], rel_bias)
    return _rms_norm(h, g_final)


def setup_inputs(seed: int = 0) -> dict:
    key = jax.random.key(seed)
    ks = jax.random.split(key, 18)
    f32 = jnp.float32
    nrm = lambda k, shp, s: jax.random.normal(k, shp, f32) * s
    return {
        "x_prompt": nrm(ks[0], (BATCH, SEQ, D_MODEL), 1.0),
        "x_sample": nrm(ks[1], (DEC_BATCH, DEC_SEQ, D_MODEL), 1.0),
        "g_norm": 1.0 + nrm(ks[2], (DEPTH, D_MODEL), 0.02),
        "w_in": nrm(ks[3], (DEPTH, D_MODEL, IN_WIDTH), D_MODEL ** -0.5),
        "lambda_q1": nrm(ks[4], (DEPTH, A_QK_DIM), 0.1),
        "lambda_k1": nrm(ks[5], (DEPTH, A_QK_DIM), 0.1),
        "lambda_q2": nrm(ks[6], (DEPTH, A_QK_DIM), 0.1),
        "lambda_k2": nrm(ks[7], (DEPTH, A_QK_DIM), 0.1),
        "subln_w": 1.0 + nrm(ks[8], (DEPTH, A_V_DIM), 0.02),
        "q_norm_b": 1.0 + nrm(ks[9], (DEPTH, B_HEAD_DIM), 0.02),
        "k_norm_b": 1.0 + nrm(ks[10], (DEPTH, B_HEAD_DIM), 0.02),
        "w_proj_a": nrm(ks[11], (DEPTH, A_WIDTH, D_MODEL), A_WIDTH ** -0.5),
        "w_proj_b": nrm(ks[12], (DEPTH, B_WIDTH, D_MODEL), B_WIDTH ** -0.5),
        "w_out": nrm(ks[13], (DEPTH, D_MODEL, D_MODEL), D_MODEL ** -0.5),
        "rel_bias": nrm(ks[14], (REL_BUCKETS, A_HEADS), 0.3),
        "g_final": 1.0 + nrm(ks[15], (D_MODEL,), 0.02),
    }


def reference(x_prompt, x_sample, g_norm, w_in, lambda_q1, lambda_k1, lambda_q2, lambda_k2, subln_w,
              q_norm_b, k_norm_b, w_proj_a, w_proj_b, w_out, rel_bias, g_final):
    y_prompt = _trunk(x_prompt, g_norm, w_in, lambda_q1, lambda_k1, lambda_q2, lambda_k2, subln_w,
                      q_norm_b, k_norm_b, w_proj_a, w_proj_b, w_out, rel_bias, g_final)
    y_sample = _trunk(x_sample, g_norm, w_in, lambda_q1, lambda_k1, lambda_q2, lambda_k2, subln_w,
                      q_norm_b, k_norm_b, w_proj_a, w_proj_b, w_out, rel_bias, g_final)
    return (y_prompt, y_sample)
```

```python
import math
from contextlib import ExitStack

import numpy as np
import concourse.bass as bass
import concourse.mybir as mybir
from concourse.bass_utils import run_bass_kernel_spmd

F32 = mybir.dt.float32
BF16 = mybir.dt.bfloat16
AF = mybir.ActivationFunctionType
ALU = mybir.AluOpType
AX = mybir.AxisListType

D = 1024
INW = 8704
EPS = 1e-6
NRING = 8
LAMBDA_INIT = 0.8 - 0.6 * math.exp(-0.3 * 0)
ATTACH_WAIT = True

FULL_CFG = dict(NP=2, SP=4096, SSK=8192, SSQ=2048)


class Ctx:
    def __init__(self, nc, es):
        self.nc = nc
        self.E = {"pe": nc.tensor, "act": nc.scalar, "dve": nc.vector, "pool": nc.gpsimd, "sp": nc.sync}
        self.sem = {}
        self.cnt = {}
        for n in self.E:
            self.sem[n] = es.enter_context(nc.semaphore("s_" + n))
            self.cnt[n] = 0
        self.seen = {n: {} for n in self.E}
        self.rings = {}
        for q in ("sp", "pool"):
            self.rings[q] = [[es.enter_context(nc.semaphore("d_%s%d" % (q, i))), 0] for i in range(NRING)]
        self.rpos = {q: 0 for q in self.rings}
        self.Tw = {}
        self.Tr = {}
        self.n_ins = 0

    def _waits(self, e, deps):
        best = {}
        for d in deps:
            if d is None:
                continue
            key, sem, val = d
            if key == e:
                if e == "pe" or self.cnt[e] - val >= 2:
                    continue
            if self.seen[e].get(key, 0) >= val:
                continue
            if key not in best or best[key][1] < val:
                best[key] = (sem, val)
        for key, (sem, val) in best.items():
            self.seen[e][key] = val
        return list(best.values())

    def _emit(self, e, fns, deps):
        eng = self.E[e]
        ws = self._waits(e, deps)
        if ATTACH_WAIT and ws:
            for sem, val in ws[:-1]:
                eng.wait_ge(sem, val)
                self.n_ins += 1
        else:
            for sem, val in ws:
                eng.wait_ge(sem, val)
                self.n_ins += 1
        ins = None
        for i, fn in enumerate(fns):
            ins = fn(eng)
            self.n_ins += 1
            if i == 0 and ATTACH_WAIT and ws:
                ins._wait_ge(*ws[-1])
        ins.then_inc(self.sem[e], 1)
        self.cnt[e] += 1
        return (e, self.sem[e], self.cnt[e])

    def _emit_dma(self, q, out, in_, deps):
        eng = self.E[q]
        idx = self.rpos[q] % NRING
        slot = self.rings[q][idx]
        self.rpos[q] += 1
        sem, val = slot
        key = "d_%s%d" % (q, idx)
        alld = list(deps)
        if val > 0:
            alld.append((key, sem, val))
        for s, v in self._waits(q, alld):
            eng.wait_ge(s, v)
            self.n_ins += 1
        eng.dma_start(out=out, in_=in_).then_inc(sem, 16)
        self.n_ins += 1
        slot[1] = val + 16
        return (key, sem, val + 16)

    def _deps(self, reads, writes, extra):
        deps = list(extra)
        for k in reads:
            deps += self.Tw.get(k, [])
        for k in writes:
            deps += self.Tw.get(k, [])
            deps += list(self.Tr.get(k, {}).values())
        return deps

    def _note(self, ev, reads, writes):
        for k in reads:
            d = self.Tr.setdefault(k, {})
            if ev[0] not in d or d[ev[0]][2] < ev[2]:
                d[ev[0]] = ev
        for k in writes:
            self.Tw[k] = [ev]
            self.Tr[k] = {}

    def run(self, e, fn, reads=(), writes=(), extra=()):
        fns = fn if isinstance(fn, (list, tuple)) else [fn]
        ev = self._emit(e, fns, self._deps(reads, writes, extra))
        self._note(ev, reads, writes)
        return ev

    def dma(self, q, out, in_, reads=(), writes=(), extra=()):
        ev = self._emit_dma(q, out, in_, self._deps(reads, writes, extra))
        self._note(ev, reads, writes)
        return ev

    def barrier(self):
        evs = [(n, self.sem[n], self.cnt[n]) for n in self.E if self.cnt[n] > 0]
        for q, ring in self.rings.items():
            for i, (sem, val) in enumerate(ring):
                if val > 0:
                    evs.append(("d_%s%d" % (q, i), sem, val))
        for e in self.E:
            for s, v in self._waits(e, evs):
                self.E[e].wait_ge(s, v)
                self.n_ins += 1
        self.Tw = {}
        self.Tr = {}


def _cap(ap, off, dims):
    return bass.AP(tensor=ap.tensor, offset=ap.offset + off, ap=[list(ap.ap[0])] + [list(d) for d in dims])


def _bcast_rows(dram_ap, off, n, parts=128):
    return bass.AP(tensor=dram_ap.tensor, offset=dram_ap.offset + off, ap=[[0, parts], [1, n]])


def _build(cfg):
    NP, SP, SSK, SSQ = cfg["NP"], cfg["SP"], cfg["SSK"], cfg["SSQ"]
    NOWN = SSQ // 128
    NO = SSK // 128 - NOWN
    SMK = max(SP, SSK)
    SMQ = max(SP, SSQ)
    nc = bass.Bass("TRN2", target_bir_lowering=False)
    dt = nc.dram_tensor

    def din(name, shape, dtype=F32):
        return dt(name, shape, dtype, kind="ExternalInput").ap()

    xp = din("xp", [NP, SP, D])
    xs = din("xs", [SSK, D])
    g_norm = din("g_norm", [1, D])
    g_final = din("g_final", [1, D])
    w_in = din("w_in", [D, INW])
    lamq = din("lamq", [4, 64])
    subln_w = din("subln_w", [1, 128])
    q_norm_b = din("q_norm_b", [1, 128])
    k_norm_b = din("k_norm_b", [1, 128])
    w_pa = din("w_pa", [D, D])
    w_pb = din("w_pb", [D, D])
    w_o = din("w_o", [D, D])
    rel_bias = din("rel_bias", [32, 8])
    ropeP = din("ropeP", [SP, 256])
    ropeS = din("ropeS", [SSK, 256])
    eye = din("eye", [128, 128])
    onehot = din("onehot", [32, 1280])
    onehot_n = din("onehot_n", [32, 640])
    onehot_p = din("onehot_p", [32, 640])
    side_in = din("side", [1, NO])
    yp = dt("yp", [NP, SP, D], F32, kind="ExternalOutput").ap()
    ys = dt("ys", [SSQ, D], F32, kind="ExternalOutput").ap()

    def dscr(name, shape, dtype):
        return dt(name, shape, dtype, kind="Internal").ap()

    QTa = dscr("QTa", [8, 128, SMQ], BF16)
    KTa = dscr("KTa", [8, 128, SMK], BF16)
    Va = dscr("Va", [SMK, 1024], BF16)
    QTb = dscr("QTb", [8, 128, SMQ], BF16)
    KTb = dscr("KTb", [2, 128, SMK], BF16)
    Vb = dscr("Vb", [SMK, 256], BF16)
    ZaT = dscr("ZaT", [8, 128, SMQ], BF16)
    ZbT = dscr("ZbT", [8, 128, SMQ], BF16)
    GaT = dscr("GaT", [8, 128, SMQ], BF16)
    GbT = dscr("GbT", [8, 128, SMQ], BF16)
    OaT = dscr("OaT", [8, 128, SMQ], F32)
    ObT = dscr("ObT", [8, 128, SMQ], F32)
    Wbf = dscr("Wbf", [3, 128, 8 * D], BF16)
    Tscr = dscr("Tscr", [8, 1280], F32)
    Tnscr = dscr("Tnscr", [8, 640], F32)
    Tpscr = dscr("Tpscr", [8, 640], F32)

    jobs = []
    for n in range(NP):
        jobs.append(dict(x=xp[n], Sk=SP, Sq=SP, rope=ropeP, out=yp[n], sample=False))
    jobs.append(dict(x=xs, Sk=SSK, Sq=SSQ, rope=ropeS, out=ys, sample=True))

    with ExitStack() as es:
        C = Ctx(nc, es)

        uid = [0]

        def SB(st, name, shape, dtype):
            uid[0] += 1
            return st.enter_context(nc.sbuf_tensor("%s_u%d" % (name, uid[0]), shape, dtype))

        def PS(st, name, shape, dtype):
            uid[0] += 1
            return st.enter_context(nc.psum_tensor("%s_u%d" % (name, uid[0]), shape, dtype))

        identb = SB(es, "identb", [128, 128], BF16)
        gn_bc = SB(es, "gn_bc", [128, D], F32)
        gf_bc = SB(es, "gf_bc", [128, D], F32)
        gq_bc = SB(es, "gq_bc", [128, 128], F32)
        gk_bc = SB(es, "gk_bc", [128, 128], F32)
        sublnc = SB(es, "sublnc", [128, 1], F32)
        ggs_q = SB(es, "ggs_q", [128, 256], F32)
        ggs_k = SB(es, "ggs_k", [128, 256], F32)
        neglam = SB(es, "neglam", [128, 1], F32)
        SelA = SB(es, "SelA", [128, 128], F32)
        SelBb = SB(es, "SelBb", [128, 128], F32)
        ones32 = SB(es, "ones32", [128, 32], BF16)
        onesf = SB(es, "onesf", [128, 128], F32)
        clo = SB(es, "clo", [128, 8], F32)
        chi = SB(es, "chi", [128, 8], F32)
        fb = SB(es, "fb", [128, 8, NO], F32)
        epsc = SB(es, "epsc", [128, 1], F32)
        zeroc = SB(es, "zeroc", [128, 1], F32)
        onec = SB(es, "onec", [128, 1], F32)

        with ExitStack() as st:
            identf = SB(st, "identf", [128, 128], F32)
            subl = SB(st, "subl", [128, 1], F32)
            lamv = SB(st, "lamv", [128, 4, 64], F32)
            lprod = SB(st, "lprod", [128, 2, 64], F32)
            lred = SB(st, "lred", [128, 2], F32)
            lexp = SB(st, "lexp", [128, 2], F32)
            rbs = SB(st, "rbs", [32, 8], F32)
            oh = SB(st, "oh", [32, 1280], F32)
            ohn = SB(st, "ohn", [32, 640], F32)
            ohp = SB(st, "ohp", [32, 640], F32)
            side = SB(st, "side", [128, NO], F32)
            dif = SB(st, "dif", [128, 8], F32)
            Tsb = SB(st, "Tsb", [8, 1280], F32)
            Tnsb = SB(st, "Tnsb", [8, 640], F32)
            Tpsb = SB(st, "Tpsb", [8, 640], F32)
            pz = PS(st, "pz", [128, 512], F32)

            C.dma("sp", identf[:], eye[:, :], writes=["identf"])
            C.dma("sp", gn_bc[:], _bcast_rows(g_norm, 0, D), writes=["gn_bc"])
            C.dma("sp", gf_bc[:], _bcast_rows(g_final, 0, D), writes=["gf_bc"])
            C.dma("sp", gq_bc[:], _bcast_rows(q_norm_b, 0, 128), writes=["gq_bc"])
            C.dma("sp", gk_bc[:], _bcast_rows(k_norm_b, 0, 128), writes=["gk_bc"])
            C.dma("sp", subl[:], bass.AP(tensor=subln_w.tensor, offset=subln_w.offset, ap=[[1, 128], [1, 1]]),
                  writes=["subl"])
            for i in range(4):
                C.dma("sp", lamv[:, i, :], _bcast_rows(lamq, 64 * i, 64), writes=[("lamv", i)])
            C.dma("sp", rbs[:], rel_bias[:, :], writes=["rbs"])
            C.dma("sp", oh[:], onehot[:, :], writes=["oh"])
            C.dma("sp", ohn[:], onehot_n[:, :], writes=["ohn"])
            C.dma("sp", ohp[:], onehot_p[:, :], writes=["ohp"])
            C.dma("sp", clo[:], _bcast_rows(rel_bias, 15 * 8, 8), writes=["clo"])
            C.dma("sp", chi[:], _bcast_rows(rel_bias, 31 * 8, 8), writes=["chi"])
            C.dma("sp", side[:], _bcast_rows(side_in, 0, NO), writes=["side"])

            C.run("dve", lambda e: e.tensor_copy(out=identb[:], in_=identf[:]), reads=["identf"], writes=["identb"])
            for (gsrc, gk, gdst, gdk) in ((gq_bc, "gq_bc", ggs_q, "ggs_q"), (gk_bc, "gk_bc", ggs_k, "ggs_k")):
                C.run("dve", lambda e: e.tensor_copy(out=gdst[:, 0:128], in_=gsrc[:]), reads=[gk], writes=[(gdk, 0)])
                for q4 in range(4):
                    src_off = (q4 ^ 1) * 32
                    C.run("dve", lambda e: e.tensor_copy(out=gdst[:, 128 + q4 * 32:128 + (q4 + 1) * 32],
                                                         in_=gsrc[:, src_off:src_off + 32]),
                          reads=[gk], writes=[(gdk, 1 + q4)])
            C.run("dve", lambda e: e.tensor_scalar(out=sublnc[:], in0=subl[:], scalar1=1.0 - LAMBDA_INIT,
                                                   scalar2=None, op0=ALU.mult), reads=["subl"], writes=["sublnc"])
            for i in range(2):
                C.run("dve", lambda e: e.tensor_tensor(out=lprod[:, i, :], in0=lamv[:, 2 * i, :],
                                                       in1=lamv[:, 2 * i + 1, :], op=ALU.mult),
                      reads=[("lamv", 2 * i), ("lamv", 2 * i + 1)], writes=[("lprod", i)])
                C.run("dve", lambda e: e.tensor_reduce(out=lred[:, i:i + 1], in_=lprod[:, i, :], axis=AX.X, op=ALU.add),
                      reads=[("lprod", i)], writes=[("lred", i)])
            C.run("act", lambda e: e.activation(out=lexp[:], in_=lred[:], func=AF.Exp),
                  reads=[("lred", 0), ("lred", 1)], writes=["lexp"])
            C.run("dve", lambda e: e.tensor_tensor(out=neglam[:], in0=lexp[:, 1:2], in1=lexp[:, 0:1], op=ALU.subtract),
                  reads=["lexp"], writes=["neglam"])
            C.run("dve", lambda e: e.tensor_scalar(out=neglam[:], in0=neglam[:], scalar1=-LAMBDA_INIT, scalar2=None,
                                                   op0=ALU.add), reads=["neglam"], writes=["neglam"])
            C.run("dve", lambda e: e.memset(SelA[:], 0.0), writes=["SelA"])
            C.run("dve", lambda e: e.memset(SelA[0:32, :], 1.0 / 32), writes=["SelA"])
            C.run("dve", lambda e: e.memset(SelA[64:96, :], 1.0 / 32), writes=["SelA"])
            C.run("dve", lambda e: e.memset(SelBb[:], 1.0 / 32), writes=["SelBb"])
            C.run("dve", lambda e: e.memset(SelBb[0:32, :], 0.0), writes=["SelBb"])
            C.run("dve", lambda e: e.memset(SelBb[64:96, :], 0.0), writes=["SelBb"])
            C.run("dve", lambda e: e.memset(ones32[:], 1.0), writes=["ones32"])
            C.run("dve", lambda e: e.memset(onesf[:], 1.0 / 128), writes=["onesf"])
            C.run("dve", lambda e: e.memset(epsc[:], EPS), writes=["epsc"])
            C.run("dve", lambda e: e.memset(zeroc[:], 0.0), writes=["zeroc"])
            C.run("dve", lambda e: e.memset(onec[:], 1.0), writes=["onec"])
            C.run("dve", lambda e: e.tensor_tensor(out=dif[:], in0=chi[:], in1=clo[:], op=ALU.subtract),
                  reads=["chi", "clo"], writes=["dif"])
            for h in range(8):
                C.run("dve", lambda e: e.tensor_scalar(out=fb[:, h, :], in0=side[:], scalar1=dif[:, h:h + 1],
                                                       scalar2=clo[:, h:h + 1], op0=ALU.mult, op1=ALU.add),
                      reads=["side", "dif", "clo"], writes=[("fb", h)])
            wstg = [SB(st, "wstg%d" % i, [128, 8, 512], F32) for i in range(2)]
            wstb = [SB(st, "wstb%d" % i, [128, 8, 512], BF16) for i in range(2)]
            wc = 0
            for wi_, wsrc in enumerate((w_pa, w_pb, w_o)):
                for i in range(2):
                    x_ = wc % 2
                    wc += 1
                    C.dma("sp", wstg[x_][:], wsrc[:, 512 * i:512 * (i + 1)].rearrange("(k p) c -> p k c", p=128),
                          writes=[("wstg", x_)])
                    C.run("pool", lambda e: e.tensor_copy(out=wstb[x_][:], in_=wstg[x_][:]),
                          reads=[("wstg", x_)], writes=[("wstb", x_)])
                    C.dma("pool", _cap(Wbf[wi_], 512 * i, [[D, 8], [1, 512]]), wstb[x_][:], reads=[("wstb", x_)])
            for (src, srck, dst, dstk, width, scr) in ((oh, "oh", Tsb, "Tsb", 1280, Tscr),
                                                       (ohn, "ohn", Tnsb, "Tnsb", 640, Tnscr),
                                                       (ohp, "ohp", Tpsb, "Tpsb", 640, Tpscr)):
                c0 = 0
                while c0 < width:
                    n = min(512, width - c0)
                    C.run("pe", lambda e: e.matmul(pz[0:8, 0:n], lhsT=rbs[0:32, 0:8], rhs=src[0:32, c0:c0 + n],
                                                   start=True, stop=True), reads=["rbs", srck], writes=["pz"])
                    C.run("dve", lambda e: e.tensor_scalar(out=dst[0:8, c0:c0 + n], in0=pz[0:8, 0:n], scalar1=8.0,
                                                           scalar2=None, op0=ALU.mult), reads=["pz"], writes=[dstk])
                    c0 += n
                C.dma("pool", scr[:, :], dst[0:8, :], reads=[dstk])
        C.barrier()

        for jb, job in enumerate(jobs):
            xkv, Sk, Sq, rope, outp, sample = job["x"], job["Sk"], job["Sq"], job["rope"], job["out"], job["sample"]
            nqt = Sq // 512
            nkb = Sk // 128

            with ExitStack() as st:
                SUB = min(Sk, 4096)
                xnT = SB(st, "xnT", [128, 8, SUB], BF16)
                xtb = [SB(st, "xt%d" % i, [128, 4, D], F32) for i in range(2)]
                junk = SB(st, "junk", [128, D], BF16)
                ssq = SB(st, "ssq", [128, 8], F32)
                lnt = SB(st, "lnt", [128, 8], F32)
                rstd = SB(st, "rstd", [128, 8], F32)
                xnb = [SB(st, "xn%d" % i, [128, D], BF16) for i in range(2)]
                wst = SB(st, "wst", [128, 8, 512], F32)
                wbb = [SB(st, "wb%d" % i, [128, 8, 512], BF16) for i in range(2)]
                stg = [SB(st, "stg%d" % i, [128, 4, 512], BF16) for i in range(2)]
                rpb = [SB(st, "rp%d" % i, [128, 4, 256], F32) for i in range(2)]
                qn_r = [SB(st, "qn%d" % i, [128, 512], F32) for i in range(2)]
                qo1_r = [SB(st, "qo1%d" % i, [128, 512], F32) for i in range(2)]
                qtm_r = [SB(st, "qtm%d" % i, [128, 512], F32) for i in range(2)]
                qrot_r = [SB(st, "qrot%d" % i, [128, 512], BF16) for i in range(2)]
                qjunk = SB(st, "qjunk", [128, 512], BF16)
                qss_r = [SB(st, "qss%d" % i, [128, 4], F32) for i in range(2)]
                qln_r = [SB(st, "qln%d" % i, [128, 4], F32) for i in range(2)]
                qrs_r = [SB(st, "qrs%d" % i, [128, 4], F32) for i in range(2)]
                pm = [PS(st, "pm%d" % i, [128, 512], F32) for i in range(4)]
                tp = [PS(st, "tp%d" % i, [128, 8, 128], BF16) for i in range(2)]
                tq = [PS(st, "tq%d" % i, [128, 8, 128], BF16) for i in range(2)]
                cnt = dict(pm=0, ev=0, stg=0, wb=0, tq=0, rp=0, rb=0)

                def evac(out_ap, in_ap, reads, writes):
                    cnt["ev"] += 1
                    if cnt["ev"] % 2 == 0:
                        return C.run("dve", lambda e: e.tensor_copy(out=out_ap, in_=in_ap), reads=reads, writes=writes)
                    return C.run("act", lambda e: e.activation(out=out_ap, in_=in_ap, func=AF.Copy),
                                 reads=reads, writes=writes)

                def rope_piece(wb, wi, ncol, dest, doff, ntiles, t0):
                    nh = ncol // 128
                    ggs = ggs_q if dest is QTb else ggs_k
                    blocks = [(tt, b) for tt in range(ntiles) for b in range(4)]
                    info = {}

                    def mm(idx):
                        tt, b = blocks[idx]
                        if b == 0:
                            ri = cnt["rp"] % 2
                            cnt["rp"] += 1
                            tok = t0 + tt * 512
                            C.dma("sp", rpb[ri][:], rope[tok:tok + 512, :].rearrange("(b p) c -> p b c", p=128),
                                  writes=[("rp", ri)])
                            C.run("pool", lambda e: e.tensor_tensor(out=rpb[ri][:], in0=rpb[ri][:],
                                                                    in1=_cap(ggs[:], 0, [[0, 4], [1, 256]]), op=ALU.mult),
                                  reads=[("rp", ri)], writes=[("rp", ri)])
                            si = cnt["stg"] % 2
                            cnt["stg"] += 1
                            info[tt] = (ri, si)
                        pi = cnt["pm"] % 4
                        cnt["pm"] += 1
                        C.run("pe", [(lambda e, k=k: e.matmul(pm[pi][:, 0:ncol],
                                                               lhsT=xnT[:, k, tt * 512 + b * 128: tt * 512 + (b + 1) * 128],
                                                               rhs=wb[:, k, 0:ncol], start=(k == 0), stop=(k == 7)))
                                     for k in range(8)],
                              reads=[("wb", wi), ("xnT", tt)], writes=[("pm", pi)])
                        return pi

                    pis = {0: mm(0)}
                    if len(blocks) > 1:
                        pis[1] = mm(1)
                    pend = []
                    for idx, (tt, b) in enumerate(blocks):
                        if idx + 2 < len(blocks):
                            pis[idx + 2] = mm(idx + 2)
                        pi = pis.pop(idx)
                        ri, si = info[tt]
                        rp = rpb[ri]
                        tok = t0 + tt * 512
                        rb_ = cnt["rb"] % 2
                        cnt["rb"] += 1
                        qn, qo1, qtm, qrot = qn_r[rb_], qo1_r[rb_], qtm_r[rb_], qrot_r[rb_]
                        qss, qln, qrs = qss_r[rb_], qln_r[rb_], qrs_r[rb_]
                        K_ = lambda nm, *a: (nm, rb_) + a
                        for h in range(nh):
                            C.run("act", lambda e: e.activation(out=qjunk[:, h * 128:(h + 1) * 128],
                                                                in_=pm[pi][:, h * 128:(h + 1) * 128],
                                                                func=AF.Square, accum_out=qss[:, h:h + 1]),
                                  reads=[("pm", pi)], writes=[("qjunk", h), K_("qss", h)])
                        C.run("act", lambda e: e.activation(out=qln[:, 0:nh], in_=qss[:, 0:nh], func=AF.Ln,
                                                            bias=epsc[:], scale=1.0 / 128),
                              reads=[K_("qss", h) for h in range(nh)], writes=[K_("qln")])
                        C.run("act", lambda e: e.activation(out=qrs[:, 0:nh], in_=qln[:, 0:nh], func=AF.Exp,
                                                            scale=-0.5), reads=[K_("qln")], writes=[K_("qrs")])
                        for h in range(0, nh, 2):
                            C.run("act", lambda e: e.activation(out=qn[:, h * 128:(h + 1) * 128],
                                                                in_=pm[pi][:, h * 128:(h + 1) * 128], func=AF.Copy,
                                                                scale=qrs[:, h:h + 1]),
                                  reads=[("pm", pi), K_("qrs")], writes=[K_("qn", h), K_("pmtok")])
                        for h in range(1, nh, 2):
                            C.run("dve", lambda e: e.tensor_scalar(out=qn[:, h * 128:(h + 1) * 128],
                                                                   in0=pm[pi][:, h * 128:(h + 1) * 128],
                                                                   scalar1=qrs[:, h:h + 1], scalar2=None, op0=ALU.mult),
                                  reads=[("pm", pi), K_("qrs"), K_("pmtok")], writes=[K_("qn", h)])
                        qnk = [K_("qn", h) for h in range(nh)]
                        qn_a = qn[:, 0:ncol]
                        rp_a = rp[:, b, :]
                        C.run("pool", lambda e: e.tensor_tensor(
                            out=_cap(qo1[:, 0:ncol], 0, [[128, nh], [1, 128]]),
                            in0=_cap(qn_a, 0, [[128, nh], [1, 128]]),
                            in1=_cap(rp_a, 0, [[0, nh], [1, 128]]), op=ALU.mult),
                              reads=qnk + [("rp", ri)], writes=[K_("qo1")])
                        C.run("dve", lambda e: e.tensor_tensor(
                            out=_cap(qtm[:, 0:ncol], 0, [[128, nh], [64, 2], [1, 32]]),
                            in0=_cap(qn_a, 32, [[128, nh], [64, 2], [1, 32]]),
                            in1=_cap(rp_a, 128, [[0, nh], [64, 2], [1, 32]]), op=ALU.mult),
                              reads=qnk + [("rp", ri)], writes=[K_("qtm", 0)])
                        C.run("dve", lambda e: e.tensor_tensor(
                            out=_cap(qtm[:, 0:ncol], 32, [[128, nh], [64, 2], [1, 32]]),
                            in0=_cap(qn_a, 0, [[128, nh], [64, 2], [1, 32]]),
                            in1=_cap(rp_a, 128 + 32, [[0, nh], [64, 2], [1, 32]]), op=ALU.mult),
                              reads=qnk + [("rp", ri)], writes=[K_("qtm", 1)])
                        C.run("pool", lambda e: e.tensor_tensor(out=qrot[:, 0:ncol], in0=qo1[:, 0:ncol],
                                                                in1=qtm[:, 0:ncol], op=ALU.add),
                              reads=[K_("qo1"), K_("qtm", 0), K_("qtm", 1)], writes=[K_("qrot")])
                        ti = cnt["tq"] % 2
                        cnt["tq"] += 1
                        C.run("pe", [(lambda e, h=h: e.transpose(tq[ti][:, h, :], qrot[:, h * 128:(h + 1) * 128],
                                                                  identb[:])) for h in range(nh)],
                              reads=[K_("qrot")], writes=[("tq", ti)])
                        def fin(si=si, b=b, ti=ti, tok=tok):
                            evac(stg[si][:, 0:nh, b * 128:(b + 1) * 128], tq[ti][:, 0:nh, :],
                                 reads=[("tq", ti)], writes=[("stg", si, b)])
                            if b == 3:
                                C.dma("pool", dest[doff:doff + nh, :, tok:tok + 512].rearrange("c p t -> p c t"),
                                      stg[si][:, 0:nh, :], reads=[("stg", si, bb) for bb in range(4)])

                        if pend:
                            pend.pop()()
                        pend.append(fin)
                    while pend:
                        pend.pop()()

                for t0 in range(0, Sk, SUB):
                    ntt = SUB // 512
                    nqtt = max(0, min(ntt, (Sq - t0) // 512))
                    pieces = []
                    for i in range(2):
                        pieces.append(("fm", 0 + 512 * i, 512, QTa, 4 * i, True))
                    for i in range(2):
                        pieces.append(("fm", 1024 + 512 * i, 512, KTa, 4 * i, False))
                    for i in range(2):
                        pieces.append(("tm", 2048 + 512 * i, 512, Va, 512 * i, False))
                    for i in range(2):
                        pieces.append(("fm", 3072 + 512 * i, 512, ZaT, 4 * i, True))
                    for i in range(2):
                        pieces.append(("rope", 4096 + 512 * i, 512, QTb, 4 * i, True))
                    pieces.append(("rope", 5120, 256, KTb, 0, False))
                    pieces.append(("tm", 5376, 256, Vb, 0, False))
                    for i in range(2):
                        pieces.append(("fm", 5632 + 512 * i, 512, ZbT, 4 * i, True))
                    for i in range(2):
                        pieces.append(("fm", 6656 + 512 * i, 512, GaT, 4 * i, True))
                    for i in range(2):
                        pieces.append(("fm", 7680 + 512 * i, 512, GbT, 4 * i, True))

                    active = [p for p in pieces if (nqtt if p[5] else ntt) > 0]

                    def load_w(pi_):
                        if pi_ >= len(active):
                            return
                        _, c0_, nc_, _, _, _ = active[pi_]
                        C.dma("sp", wst[:, :, 0:nc_], w_in[:, c0_:c0_ + nc_].rearrange("(k p) c -> p k c", p=128),
                              writes=["wst"])
                        C.run("pool", lambda e: e.tensor_copy(out=wbb[pi_ % 2][:, :, 0:nc_], in_=wst[:, :, 0:nc_]),
                              reads=["wst"], writes=[("wb", pi_ % 2)])


                    def fm_tile(wb, wi, dest, doff, tt):
                        tok = t0 + tt * 512
                        si = cnt["stg"] % 2
                        cnt["stg"] += 1
                        for s_ in range(4):
                            pi = cnt["pm"] % 4
                            cnt["pm"] += 1
                            C.run("pe", [(lambda e, k=k: e.matmul(pm[pi][:], lhsT=wb[:, k, s_ * 128:(s_ + 1) * 128],
                                                                   rhs=xnT[:, k, tt * 512:(tt + 1) * 512],
                                                                   start=(k == 0), stop=(k == 7)))
                                         for k in range(8)],
                                  reads=[("wb", wi), ("xnT", tt)], writes=[("pm", pi)])
                            evac(stg[si][:, s_, :], pm[pi][:], reads=[("pm", pi)], writes=[("stg", si, s_)])
                        C.dma("pool", dest[doff:doff + 4, :, tok:tok + 512].rearrange("c p t -> p c t"),
                              stg[si][:], reads=[("stg", si, s_) for s_ in range(4)])

                    ov = active[0][0] == "fm"
                    ov_n = (nqtt if active[0][5] else ntt) if ov else 0
                    load_w(0)
                    for tt in range(ntt):
                        xt = xtb[tt % 2]
                        xk = ("xt", tt % 2)
                        xsrc = xkv[t0 + tt * 512:t0 + (tt + 1) * 512, :].rearrange("(b p) c -> p b c", p=128)
                        C.dma("sp", xt[:, 0:2, :], xsrc[:, 0:2, :], writes=[xk + (0,)])
                        C.dma("pool", xt[:, 2:4, :], xsrc[:, 2:4, :], writes=[xk + (1,)])
                        for b in range(4):
                            blk = tt * 4 + b
                            col = blk % 8
                            C.run("act", lambda e: e.activation(out=junk[:], in_=xt[:, b, :], func=AF.Square,
                                                                accum_out=ssq[:, col:col + 1]),
                                  reads=[xk + (b // 2,)], writes=["junk", ("ssq", col)])
                            C.run("act", lambda e: e.activation(out=lnt[:, col:col + 1], in_=ssq[:, col:col + 1],
                                                                func=AF.Ln, bias=epsc[:], scale=1.0 / D),
                                  reads=[("ssq", col)], writes=[("lnt", col)])
                            C.run("act", lambda e: e.activation(out=rstd[:, col:col + 1], in_=lnt[:, col:col + 1],
                                                                func=AF.Exp, scale=-0.5),
                                  reads=[("lnt", col)], writes=[("rstd", col)])
                            xn = xnb[blk % 2]
                            C.run("dve", lambda e: e.scalar_tensor_tensor(out=xn[:], in0=xt[:, b, :],
                                                                          scalar=rstd[:, col:col + 1], in1=gn_bc[:],
                                                                          op0=ALU.mult, op1=ALU.mult),
                                  reads=[xk + (b // 2,), ("rstd", col)], writes=[("xn", blk % 2)])
                            tpp = tp[blk % 2]
                            C.run("pe", [(lambda e, c=c: e.transpose(tpp[:, c, :], xn[:, c * 128:(c + 1) * 128], identb[:]))
                                         for c in range(8)], reads=[("xn", blk % 2)], writes=[("tp", blk % 2)])
                            evac(xnT[:, :, tt * 512 + b * 128: tt * 512 + (b + 1) * 128], tpp[:, :, :],
                                 reads=[("tp", blk % 2)], writes=[("xnT", tt)])
                        if ov and tt == 1:
                            load_w(1)
                        if ov and 1 <= tt <= ov_n:
                            fm_tile(wbb[0], 0, active[0][3], active[0][4], tt - 1)
                    if ov:
                        if ntt < 2:
                            load_w(1)
                        for tt in range(max(ntt - 1, 0), ov_n):
                            fm_tile(wbb[0], 0, active[0][3], active[0][4], tt)

                    for pidx, (kind, col0, ncol, dest, doff, qside) in enumerate(active):
                        ntiles = nqtt if qside else ntt
                        wi = pidx % 2
                        wb = wbb[wi]
                        if ov and pidx == 0:
                            continue
                        if not ov and pidx == 0:
                            load_w(0)
                        load_w(pidx + 1)
                        if kind == "rope":
                            rope_piece(wb, wi, ncol, dest, doff, ntiles, t0)
                            continue
                        for tt in range(ntiles):
                            tok = t0 + tt * 512
                            if kind == "fm":
                                fm_tile(wb, wi, dest, doff, tt)
                            elif kind == "tm":
                                si = cnt["stg"] % 2
                                cnt["stg"] += 1
                                for b in range(4):
                                    pi = cnt["pm"] % 4
                                    cnt["pm"] += 1
                                    C.run("pe", [(lambda e, k=k: e.matmul(pm[pi][:, 0:ncol],
                                                                           lhsT=xnT[:, k, tt * 512 + b * 128: tt * 512 + (b + 1) * 128],
                                                                           rhs=wb[:, k, 0:ncol],
                                                                           start=(k == 0), stop=(k == 7)))
                                                 for k in range(8)],
                                          reads=[("wb", wi), ("xnT", tt)], writes=[("pm", pi)])
                                    evac(stg[si][:, b, 0:ncol], pm[pi][:, 0:ncol], reads=[("pm", pi)],
                                         writes=[("stg", si, b)])
                                C.dma("pool", dest[tok:tok + 512, doff:doff + ncol].rearrange("(b p) c -> p b c", p=128),
                                      stg[si][:, :, 0:ncol], reads=[("stg", si, b) for b in range(4)])
            C.barrier()

            jw = ExitStack()
            Wpa = SB(jw, "Wpa", [128, 8, D], BF16)
            Wpb = SB(jw, "Wpb", [128, 8, D], BF16)
            Wo = SB(jw, "Wo", [128, 8, D], BF16)

            with ExitStack() as st:
                QTt = [[SB(st, "QT%d_%d" % (i, m), [128, Sq], BF16) for m in range(2)] for i in range(2)]
                KTt = [SB(st, "KT%d" % i, [128, Sk], BF16) for i in range(2)]
                Vt = [SB(st, "V%d" % i, [128, nkb, 128], BF16) for i in range(2)]
                Grt = [SB(st, "Gr%d" % i, [128, 1152], F32) for i in range(2)]
                Gnt = [SB(st, "Gn%d" % i, [128, 512], F32) for i in range(2)]
                Gpt = [SB(st, "Gp%d" % i, [128, 512], F32) for i in range(2)]
                ebuf = [SB(st, "e%d" % i, [128, 2, 512], BF16) for i in range(4)]
                tmpb = [SB(st, "tmpb%d" % i, [128, 2, 512], F32) for i in range(2)]
                Osb = [SB(st, "Osb%d" % i, [128, 2, 512], F32) for i in range(2)]
                ssb = [SB(st, "ssb%d" % i, [128, 512], F32) for i in range(2)]
                rsb = [SB(st, "rsb%d" % i, [128, 512], F32) for i in range(2)]
                ob = [SB(st, "ob%d" % i, [128, 512], F32) for i in range(2)]
                ob2 = [SB(st, "ob2_%d" % i, [128, 512], F32) for i in range(2)]
                sc = [PS(st, "sc%d" % i, [128, 2, 512], F32) for i in range(2)]
                Ops = PS(st, "Ops", [128, 2, 512], F32)
                smp = PS(st, "smp", [128, 512], F32)
                bcp = PS(st, "bcp", [128, 512], F32)

                hus = [("a", h) for h in range(8)] + [("b", n, pr) for n in range(2) for pr in range(2)]
                units = []
                for hi, hu in enumerate(hus):
                    for i in range(nqt):
                        for j in range(nkb):
                            units.append((hi, i, j))
                loaded = {}

                def load_hu(hi):
                    if hi in loaded or hi >= len(hus):
                        return
                    loaded[hi] = True
                    hu = hus[hi]
                    s = hi % 2
                    if hu[0] == "a":
                        h = hu[1]
                        C.dma("sp", QTt[s][0][:], QTa[h, :, 0:Sq], writes=[("QT", s, 0)])
                        C.dma("sp", KTt[s][:], KTa[h, :, 0:Sk], writes=[("KT", s)])
                        C.dma("sp", Vt[s][:], Va[0:Sk, h * 128:(h + 1) * 128].rearrange("(j p) d -> p j d", p=128),
                              writes=[("V", s)])
                        C.dma("sp", Grt[s][:], bass.AP(tensor=Tscr.tensor, offset=Tscr.offset + h * 1280,
                                                      ap=[[1, 128], [1, 1152]]), writes=[("Gr", s)])
                        if sample:
                            C.dma("sp", Gnt[s][:], bass.AP(tensor=Tnscr.tensor, offset=Tnscr.offset + h * 640,
                                                          ap=[[1, 128], [1, 512]]), writes=[("Gn", s)])
                            C.dma("sp", Gpt[s][:], bass.AP(tensor=Tpscr.tensor, offset=Tpscr.offset + h * 640,
                                                          ap=[[1, 128], [1, 512]]), writes=[("Gp", s)])
                    else:
                        n, pr = hu[1], hu[2]
                        g0 = n * 4 + 2 * pr
                        C.dma("sp", QTt[s][0][:], QTb[g0, :, 0:Sq], writes=[("QT", s, 0)])
                        C.dma("sp", QTt[s][1][:], QTb[g0 + 1, :, 0:Sq], writes=[("QT", s, 1)])
                        C.dma("sp", KTt[s][:], KTb[n, :, 0:Sk], writes=[("KT", s)])
                        C.dma("sp", Vt[s][:], Vb[0:Sk, n * 128:(n + 1) * 128].rearrange("(j p) d -> p j d", p=128),
                              writes=[("V", s)])

                def bias_mode(hu, s, i, j):
                    if hu[0] == "b":
                        return ("far", zeroc[:, 0:1], [])
                    h = hu[1]
                    if (not sample) or j < NOWN:
                        d = j - 4 * i
                        if -1 <= d <= 4:
                            return ("near", _cap(Grt[s][:], 639 + 128 * d, [[0, 2], [-1, 512]]), [("Gr", s)])
                        return ("far", (clo if d < -1 else chi)[:, h:h + 1], [])
                    if j == NOWN and i == nqt - 1:
                        return ("near", _cap(Gnt[s][:], 511, [[0, 2], [-1, 512]]), [("Gn", s)])
                    if j == nkb - 1 and i == 0:
                        return ("near", _cap(Gpt[s][:], 511, [[0, 2], [-1, 512]]), [("Gp", s)])
                    return ("far", fb[:, h, j - NOWN:j - NOWN + 1], [])

                def emit_qk(u):
                    hi, i, j = units[u]
                    hu = hus[hi]
                    s = hi % 2
                    scp = sc[u % 2]
                    q0 = i * 512
                    if hu[0] == "a":
                        fns = [lambda e: e.matmul(scp[:, 0, :], lhsT=KTt[s][0:64, j * 128:(j + 1) * 128],
                                                  rhs=QTt[s][0][0:64, q0:q0 + 512], start=True, stop=True,
                                                  tile_position=(0, 0)),
                               lambda e: e.matmul(scp[:, 1, :], lhsT=KTt[s][64:128, j * 128:(j + 1) * 128],
                                                  rhs=QTt[s][0][64:128, q0:q0 + 512], start=True, stop=True,
                                                  tile_position=(64, 0))]
                        rd = [("QT", s, 0), ("KT", s)]
                    else:
                        fns = [lambda e: e.matmul(scp[:, 0, :], lhsT=KTt[s][:, j * 128:(j + 1) * 128],
                                                  rhs=QTt[s][0][:, q0:q0 + 512], start=True, stop=True),
                               lambda e: e.matmul(scp[:, 1, :], lhsT=KTt[s][:, j * 128:(j + 1) * 128],
                                                  rhs=QTt[s][1][:, q0:q0 + 512], start=True, stop=True)]
                        rd = [("QT", s, 0), ("QT", s, 1), ("KT", s)]
                    C.run("pe", fns, reads=rd, writes=[("sc", u % 2)])

                deferred = {}

                def emit_epilogue(u, tcount):
                    hi, i, j = units[u]
                    hu = hus[hi]
                    x = tcount % 2
                    q0 = i * 512
                    C.run("dve", lambda e: e.tensor_copy(out=Osb[x][:], in_=Ops[:]), reads=["Ops"], writes=[("Osb", x)])
                    C.run("dve", lambda e: e.tensor_copy(out=ssb[x][:], in_=smp[:]), reads=["smp"],
                          writes=[("ssb", x)])
                    isa = hu[0] == "a"

                    def part1():
                        C.run("pe", lambda e: e.matmul(bcp[:], lhsT=SelA[:], rhs=ssb[x][:], start=True, stop=True),
                              reads=[("ssb", x)], writes=["bcp"])
                        C.run("dve", lambda e: e.reciprocal(out=rsb[x][:], in_=bcp[:]), reads=["bcp"], writes=[("rsb", x)])
                        C.run("dve", lambda e: e.scalar_tensor_tensor(out=ob[x][:], in0=Osb[x][:, 0, :], scalar=1.0,
                                                                      in1=rsb[x][:], op0=ALU.mult, op1=ALU.mult),
                              reads=[("Osb", x), ("rsb", x)], writes=[("ob", x)])

                    def part2():
                        C.run("pe", lambda e: e.matmul(bcp[:], lhsT=SelBb[:], rhs=ssb[x][:], start=True, stop=True),
                              reads=[("ssb", x)], writes=["bcp"])
                        C.run("dve", lambda e: e.reciprocal(out=rsb[x][:], in_=bcp[:]), reads=["bcp"], writes=[("rsb", x)])
                        C.run("dve", lambda e: e.scalar_tensor_tensor(out=ob2[x][:], in0=Osb[x][:, 1, :], scalar=1.0,
                                                                      in1=rsb[x][:], op0=ALU.mult, op1=ALU.mult),
                              reads=[("Osb", x), ("rsb", x)], writes=[("ob2", x)])
                        if isa:
                            C.run("dve", lambda e: e.scalar_tensor_tensor(out=ob[x][:], in0=ob2[x][:], scalar=neglam[:, 0:1],
                                                                          in1=ob[x][:], op0=ALU.mult, op1=ALU.add),
                                  reads=[("ob", x), ("ob2", x)], writes=[("ob", x)])
                            C.dma("pool", OaT[hu[1], :, q0:q0 + 512], ob[x][:], reads=[("ob", x)])
                        else:
                            g0 = hu[1] * 4 + 2 * hu[2]
                            C.dma("pool", ObT[g0, :, q0:q0 + 512], ob[x][:], reads=[("ob", x)])
                            C.dma("pool", ObT[g0 + 1, :, q0:q0 + 512], ob2[x][:], reads=[("ob2", x)])

                    p1, p2 = 2, 4
                    if nkb >= 24 and u + 1 < NU:
                        hi2, i2, _ = units[u + 1]
                        if hus[hi2][0] == "b" or 4 * i2 - 1 >= 14:
                            p1, p2 = 2, 6
                        else:
                            p1, p2 = 4 * i2 + 7, 4 * i2 + 11
                    deferred.setdefault(u + 1 + p1, []).append(part1)
                    deferred.setdefault(u + 1 + p2, []).append(part2)

                NU = len(units)
                load_hu(0)
                load_hu(1)
                emit_qk(0)
                if NU > 1:
                    emit_qk(1)
                for wi_, wdst in enumerate((Wpa, Wpb, Wo)):
                    C.dma("sp", wdst[:], Wbf[wi_].rearrange("p (k c) -> p k c", k=8))
                tcount = 0
                for u in range(NU):
                    hi, i, j = units[u]
                    hu = hus[hi]
                    s = hi % 2
                    if i == 0 and j == 0 and u > 0:
                        pass
                    mode, bap, brd = bias_mode(hu, s, i, j)
                    scale = 0.125 if hu[0] == "a" else 128 ** -0.5
                    eb = ebuf[u % 4]
                    if mode == "near":
                        tb = tmpb[u % 2]
                        C.run("dve", lambda e: e.tensor_tensor(out=tb[:], in0=sc[u % 2][:], in1=bap, op=ALU.add),
                              reads=[("sc", u % 2)] + brd, writes=[("tmpb", u % 2)])
                        C.run("act", lambda e: e.activation(out=eb[:], in_=tb[:], func=AF.Exp, bias=zeroc[:, 0:1],
                                                            scale=scale),
                              reads=[("tmpb", u % 2)], writes=[("e", u % 4)])
                    else:
                        C.run("act", lambda e: e.activation(out=eb[:], in_=sc[u % 2][:], func=AF.Exp, bias=bap, scale=scale),
                              reads=[("sc", u % 2)], writes=[("e", u % 4)])
                    first = j == 0
                    last = j == nkb - 1
                    if u + 2 < NU:
                        emit_qk(u + 2)
                    fns = [lambda e: e.matmul(Ops[:, 0, :], lhsT=Vt[s][:, j, :], rhs=eb[:, 0, :], start=first, stop=last),
                           lambda e: e.matmul(Ops[:, 1, :], lhsT=Vt[s][:, j, :], rhs=eb[:, 1, :], start=first, stop=last)]
                    rds = [("e", u % 4), ("V", s)]
                    wrs = ["Ops"]
                    if j % 2 == 1:
                        ebp = ebuf[(u - 1) % 4]
                        sf = j == 1
                        for qd, src in enumerate((ebp[:, 0, :], ebp[:, 1, :], eb[:, 0, :], eb[:, 1, :])):
                            fns.append(lambda e, qd=qd, src=src: e.matmul(smp[32 * qd:32 * qd + 32, :], lhsT=ones32[:, 0:32],
                                                                          rhs=src, start=sf, stop=last,
                                                                          tile_position=(0, 32 * qd)))
                        rds.append(("e", (u - 1) % 4))
                        wrs.append("smp")
                    C.run("pe", fns, reads=rds, writes=wrs)
                    if last:
                        emit_epilogue(u, tcount)
                        tcount += 1
                        if i == nqt - 1:
                            load_hu(hi + 2)
                    for fn in deferred.pop(u, []):
                        fn()
                for k in sorted(deferred):
                    for fn in deferred[k]:
                        fn()
            C.barrier()

            with ExitStack() as st:
                Ob = [SB(st, "Ob%d" % i, [128, 2, 512], F32) for i in range(3)]
                Zb = [SB(st, "Zb%d" % i, [128, 2, 512], BF16) for i in range(3)]
                U = SB(st, "U", [128, 16, 512], BF16)
                U2 = SB(st, "U2", [128, 16, 512], BF16)
                Gt = [SB(st, "Gt%d" % i, [128, 2, 512], BF16) for i in range(2)]
                mg = SB(st, "mg", [128, 8, 512], BF16)
                xt3 = SB(st, "xt3", [128, 4, D], F32)
                sqr = [SB(st, "sq%d" % i, [128, 2, 512], F32) for i in range(3)]
                lnr = [SB(st, "ln%d" % i, [128, 2, 512], F32) for i in range(3)]
                ezr = [SB(st, "ez%d" % i, [128, 2, 512], F32) for i in range(3)]
                egr = [SB(st, "eg%d" % i, [128, 2, 512], F32) for i in range(2)]
                m1 = SB(st, "m1", [128, 512], F32)
                m2 = SB(st, "m2", [128, 512], F32)
                hres = [SB(st, "hres%d" % i, [128, D], F32) for i in range(2)]
                yst = [SB(st, "yst%d" % i, [128, D], F32) for i in range(2)]
                junk3 = SB(st, "junk3", [128, D], BF16)
                fss = SB(st, "fss", [128, 2], F32)
                fln = SB(st, "fln", [128, 2], F32)
                frs = SB(st, "frs", [128, 2], F32)
                pp = PS(st, "pp", [128, 2, 512], F32)
                yap = [PS(st, "yap%d" % i, [128, 512], F32) for i in range(2)]
                ybp = [PS(st, "ybp%d" % i, [128, 512], F32) for i in range(2)]
                opsb = [PS(st, "opsb%d" % i, [128, 512], F32) for i in range(2)]

                Ub = [U, U2]

                def tile_items(tt):
                    tok = tt * 512
                    Uc = Ub[tt % 2]
                    UK = lambda hd: ("U", tt % 2, hd)

                    def stage0(hp):
                        isa = hp < 4
                        o = hp % 3
                        for m in range(2):
                            hh = (2 * hp + m) % 8
                            C.dma("sp", Ob[o][:, m, :], (OaT if isa else ObT)[hh, :, tok:tok + 512], writes=[("Ob", o, m)])
                            C.dma("sp", Zb[o][:, m, :], (ZaT if isa else ZbT)[hh, :, tok:tok + 512], writes=[("Zb", o, m)])

                    def stage1(hp, r):
                        isa = hp < 4
                        o = hp % 3
                        OK_ = [("Ob", o, 0), ("Ob", o, 1)]
                        ZK_ = [("Zb", o, 0), ("Zb", o, 1)]
                        ez = ezr[r]
                        C.run("act", lambda e: e.activation(out=ez[:], in_=Zb[o][:], func=AF.Exp, scale=-1.0),
                              reads=ZK_, writes=[("ez", r)])
                        C.run("act", lambda e: e.activation(out=ez[:], in_=ez[:], func=AF.Ln, bias=onec[:], scale=1.0),
                              reads=[("ez", r)], writes=[("ez", r)])
                        C.run("act", lambda e: e.activation(out=ez[:], in_=ez[:], func=AF.Exp, scale=-1.0),
                              reads=[("ez", r)], writes=[("ez", r)])
                        C.run("pool", lambda e: e.tensor_tensor(out=ez[:], in0=Zb[o][:], in1=ez[:], op=ALU.mult),
                              reads=ZK_ + [("ez", r)], writes=[("ez", r)])
                        if isa:
                            C.run("pool", lambda e: e.tensor_tensor(out=sqr[r][:], in0=Ob[o][:], in1=Ob[o][:], op=ALU.mult),
                                  reads=OK_, writes=[("sq", r)])

                    def stage2(hp, r):
                        isa = hp < 4
                        o = hp % 3
                        OK_ = [("Ob", o, 0), ("Ob", o, 1)]
                        ez = ezr[r]
                        if isa:
                            sq = sqr[r]
                            ln_ = lnr[r]
                            C.run("pe", [lambda e: e.matmul(pp[:, 0, :], lhsT=onesf[:], rhs=sq[:, 0, :], start=True, stop=True),
                                         lambda e: e.matmul(pp[:, 1, :], lhsT=onesf[:], rhs=sq[:, 1, :], start=True, stop=True)],
                                  reads=[("sq", r)], writes=["pp"])
                            C.run("act", lambda e: e.activation(out=ln_[:], in_=pp[:], func=AF.Ln, bias=epsc[:], scale=1.0),
                                  reads=["pp"], writes=[("ln", r)])
                            C.run("act", lambda e: e.activation(out=ln_[:], in_=ln_[:], func=AF.Exp, scale=-0.5),
                                  reads=[("ln", r)], writes=[("ln", r)])
                            C.run("dve", lambda e: e.scalar_tensor_tensor(out=sq[:], in0=Ob[o][:], scalar=1.0, in1=ln_[:],
                                                                          op0=ALU.mult, op1=ALU.mult),
                                  reads=OK_ + [("ln", r)], writes=[("sq", r)])
                            C.run("dve", lambda e: e.scalar_tensor_tensor(out=Uc[:, 2 * hp:2 * hp + 2, :], in0=sq[:],
                                                                          scalar=sublnc[:, 0:1], in1=ez[:],
                                                                          op0=ALU.mult, op1=ALU.mult),
                                  reads=[("sq", r), ("ez", r)], writes=[UK(2 * hp), UK(2 * hp + 1)])
                        else:
                            C.run("dve", lambda e: e.scalar_tensor_tensor(out=Uc[:, 2 * hp:2 * hp + 2, :], in0=Ob[o][:],
                                                                          scalar=1.0, in1=ez[:], op0=ALU.mult, op1=ALU.mult),
                                  reads=OK_ + [("ez", r)], writes=[UK(2 * hp), UK(2 * hp + 1)])

                    def a_iter(hp):
                        if hp < 8:
                            stage0(hp)
                        if 1 <= hp <= 8:
                            stage1(hp - 1, (hp - 1) % 3)
                        if hp >= 2:
                            stage2(hp - 2, (hp - 2) % 3)

                    def b_item(c):
                        gi = c % 2
                        if c == 0:
                            C.dma("sp", xt3[:], xkv[tok:tok + 512, :].rearrange("(b p) c -> p b c", p=128), writes=["xt3"])
                        C.dma("sp", Gt[gi][:, 0, :], GaT[c, :, tok:tok + 512], writes=[("Gt", gi, 0)])
                        C.dma("sp", Gt[gi][:, 1, :], GbT[c, :, tok:tok + 512], writes=[("Gt", gi, 1)])
                        C.run("pe", [(lambda e, h=h: e.matmul(yap[gi][:], lhsT=Wpa[:, h, c * 128:(c + 1) * 128], rhs=Uc[:, h, :],
                                                               start=(h == 0), stop=(h == 7))) for h in range(8)],
                              reads=[("Wpa", 0), ("Wpa", 1)] + [UK(h) for h in range(8)], writes=[("yap", gi)])
                        C.run("pe", [(lambda e, h=h: e.matmul(ybp[gi][:], lhsT=Wpb[:, h, c * 128:(c + 1) * 128],
                                                               rhs=Uc[:, 8 + h, :], start=(h == 0), stop=(h == 7)))
                                     for h in range(8)],
                              reads=[("Wpb", 0), ("Wpb", 1)] + [UK(8 + h) for h in range(8)], writes=[("ybp", gi)])
                        eg = egr[gi]
                        C.run("act", lambda e: e.activation(out=eg[:], in_=Gt[gi][:], func=AF.Exp, scale=-1.0),
                              reads=[("Gt", gi, 0), ("Gt", gi, 1)], writes=[("eg", gi)])
                        C.run("act", lambda e: e.activation(out=eg[:], in_=eg[:], func=AF.Ln, bias=onec[:], scale=1.0),
                              reads=[("eg", gi)], writes=[("eg", gi)])
                        C.run("act", lambda e: e.activation(out=eg[:], in_=eg[:], func=AF.Exp, scale=-1.0),
                              reads=[("eg", gi)], writes=[("eg", gi)])
                        C.run("dve", lambda e: e.tensor_tensor(out=m1[:], in0=eg[:, 0, :], in1=yap[gi][:], op=ALU.mult),
                              reads=[("eg", gi), ("yap", gi)], writes=["m1"])
                        C.run("dve", lambda e: e.tensor_tensor(out=m2[:], in0=eg[:, 1, :], in1=ybp[gi][:], op=ALU.mult),
                              reads=[("eg", gi), ("ybp", gi)], writes=["m2"])
                        C.run("dve", lambda e: e.scalar_tensor_tensor(out=mg[:, c, :], in0=m1[:], scalar=1.0, in1=m2[:],
                                                                      op0=ALU.mult, op1=ALU.add),
                              reads=["m1", "m2"], writes=[("mg", c)])

                    def c_item(b):
                        hi_ = b % 2
                        for half in range(2):
                            oi = (2 * b + half) % 2
                            C.run("pe", [(lambda e, c=c: e.matmul(opsb[oi][:], lhsT=mg[:, c, b * 128:(b + 1) * 128],
                                                                   rhs=Wo[:, c, half * 512:(half + 1) * 512],
                                                                   start=(c == 0), stop=(c == 7))) for c in range(8)],
                                  reads=[("mg", c) for c in range(8)] + [("Wo", half)], writes=[("opsb", oi)])
                            C.run("dve", lambda e: e.tensor_tensor(out=hres[hi_][:, half * 512:(half + 1) * 512],
                                                                   in0=xt3[:, b, half * 512:(half + 1) * 512],
                                                                   in1=opsb[oi][:], op=ALU.add),
                                  reads=["xt3", ("opsb", oi)], writes=[("hres", hi_, half)])
                        C.run("act", lambda e: e.activation(out=junk3[:], in_=hres[hi_][:], func=AF.Square,
                                                            accum_out=fss[:, hi_:hi_ + 1]),
                              reads=[("hres", hi_, 0), ("hres", hi_, 1)], writes=["junk3", ("fss", hi_)])
                        C.run("act", lambda e: e.activation(out=fln[:, hi_:hi_ + 1], in_=fss[:, hi_:hi_ + 1], func=AF.Ln,
                                                            bias=epsc[:], scale=1.0 / D),
                              reads=[("fss", hi_)], writes=[("fln", hi_)])
                        C.run("act", lambda e: e.activation(out=frs[:, hi_:hi_ + 1], in_=fln[:, hi_:hi_ + 1], func=AF.Exp,
                                                            scale=-0.5), reads=[("fln", hi_)], writes=[("frs", hi_)])
                        C.run("dve", lambda e: e.scalar_tensor_tensor(out=yst[hi_][:], in0=hres[hi_][:],
                                                                      scalar=frs[:, hi_:hi_ + 1], in1=gf_bc[:],
                                                                      op0=ALU.mult, op1=ALU.mult),
                              reads=[("hres", hi_, 0), ("hres", hi_, 1), ("frs", hi_)], writes=[("yst", hi_)])
                        C.dma("pool", outp[tok + b * 128:tok + (b + 1) * 128, :], yst[hi_][:], reads=[("yst", hi_)])

                    return ([(lambda hp=hp: a_iter(hp)) for hp in range(10)],
                            [(lambda c=c: b_item(c)) for c in range(8)] + [(lambda b=b: c_item(b)) for b in range(4)])

                items = [tile_items(tt) for tt in range(nqt)]
                for f in items[0][0]:
                    f()
                for tt in range(nqt):
                    nxt = items[tt + 1][0] if tt + 1 < nqt else []
                    bc = items[tt][1]
                    for k, f in enumerate(bc):
                        f()
                        if k < len(nxt):
                            nxt[k]()
                    for f in nxt[len(bc):]:
                        f()
            C.barrier()
            jw.close()
        nc._n_emitted = C.n_ins
    return nc


def _rel_bucket_np(rel):
    half, max_exact = 16, 8
    ret = (rel > 0).astype(np.int32) * half
    n = np.abs(rel)
    nf = np.maximum(n, 1).astype(np.float32)
    large = max_exact + (np.log(nf / np.float32(max_exact)) / np.float32(math.log(128 / max_exact))
                         * np.float32(half - max_exact)).astype(np.int32)
    large = np.minimum(large, half - 1)
    return ret + np.where(n < max_exact, n, large)


def _rope_table(pos):
    pos = np.asarray(pos)
    row = (pos // 64).astype(np.float32)
    col = (pos % 64).astype(np.float32)
    inv = (np.float32(10000.0) ** (-np.arange(0, 64, 2, dtype=np.float32) / np.float32(64))).astype(np.float32)
    ar = (row[:, None] * inv[None, :]).astype(np.float32)
    ac = (col[:, None] * inv[None, :]).astype(np.float32)
    cr, sr, cc, sc = np.cos(ar), np.sin(ar), np.cos(ac), np.sin(ac)
    Ct = np.concatenate([cr, cr, cc, cc], axis=1)
    Sg = np.concatenate([-sr, sr, -sc, sc], axis=1)
    return np.ascontiguousarray(np.concatenate([Ct, Sg], axis=1).astype(np.float32))


def _onehots(r, nq):
    m = np.arange(1280)
    bk = _rel_bucket_np(m - 639)
    oh = np.zeros((32, 1280), np.float32)
    oh[bk[:1279], m[:1279]] = 1.0
    if r < nq - 1:
        ohn = np.ascontiguousarray(oh[:, 640:1280])
    else:
        ohn = np.zeros((32, 640), np.float32)
        ohn[15, :] = 1.0
    if r > 0:
        ohp = np.ascontiguousarray(oh[:, 0:640])
    else:
        ohp = np.zeros((32, 640), np.float32)
        ohp[31, :] = 1.0
    return oh, ohn, ohp


def _run(cfg, x_prompt, x_sample, g_norm, w_in, lambda_q1, lambda_k1, lambda_q2, lambda_k2, subln_w,
         q_norm_b, k_norm_b, w_proj_a, w_proj_b, w_out, rel_bias, g_final, n_cores=8):
    NP, SP, SSK, SSQ = cfg["NP"], cfg["SP"], cfg["SSK"], cfg["SSQ"]
    nq = SSK // SSQ
    f = lambda a: np.ascontiguousarray(np.asarray(a, dtype=np.float32))
    x_prompt, x_sample = f(x_prompt), f(x_sample)
    nc = _build(cfg)
    common = {
        "g_norm": f(g_norm).reshape(1, D), "g_final": f(g_final).reshape(1, D), "w_in": f(w_in).reshape(D, INW),
        "lamq": np.ascontiguousarray(np.stack([f(lambda_q1).reshape(64), f(lambda_k1).reshape(64),
                                               f(lambda_q2).reshape(64), f(lambda_k2).reshape(64)])),
        "subln_w": f(subln_w).reshape(1, 128), "q_norm_b": f(q_norm_b).reshape(1, 128),
        "k_norm_b": f(k_norm_b).reshape(1, 128), "w_pa": f(w_proj_a).reshape(D, D), "w_pb": f(w_proj_b).reshape(D, D),
        "w_o": f(w_out).reshape(D, D), "rel_bias": f(rel_bias).reshape(32, 8),
        "ropeP": _rope_table(np.arange(SP)), "eye": np.eye(128, dtype=np.float32),
    }
    nown = SSQ // 128
    nkb = SSK // 128
    in_maps = []
    for c in range(n_cores):
        r = c % nq
        sidx = c // nq
        pos = (np.arange(SSK) + r * SSQ) % SSK
        oh, ohn, ohp = _onehots(r, nq)
        jl = np.arange(nown, nkb)
        side = ((r * nown + jl) < nkb).astype(np.float32).reshape(1, -1)
        m = dict(common)
        m["xp"] = np.ascontiguousarray(x_prompt[NP * c:NP * (c + 1)])
        m["xs"] = np.ascontiguousarray(x_sample[sidx][pos])
        m["ropeS"] = _rope_table(pos)
        m["onehot"] = oh
        m["onehot_n"] = ohn
        m["onehot_p"] = ohp
        m["side"] = np.ascontiguousarray(side)
        in_maps.append(m)
    res = run_bass_kernel_spmd(nc, in_maps, core_ids=list(range(n_cores)))
    yp = np.concatenate([res.results[c]["yp"] for c in range(n_cores)], axis=0)
    ys = np.zeros_like(x_sample)
    for c in range(n_cores):
        r = c % nq
        ys[c // nq, r * SSQ:(r + 1) * SSQ] = res.results[c]["ys"]
    return yp.astype(np.float32), ys.astype(np.float32)


def kernel(x_prompt, x_sample, g_norm, w_in, lambda_q1, lambda_k1, lambda_q2, lambda_k2, subln_w,
           q_norm_b, k_norm_b, w_proj_a, w_proj_b, w_out, rel_bias, g_final):
    return _run(FULL_CFG, x_prompt, x_sample, g_norm, w_in, lambda_q1, lambda_k1, lambda_q2, lambda_k2, subln_w,
                q_norm_b, k_norm_b, w_proj_a, w_proj_b, w_out, rel_bias, g_final)
```

```python
import math
from contextlib import ExitStack

import numpy as np
import concourse.bass as bass
import concourse.mybir as mybir
from concourse.bass_utils import run_bass_kernel_spmd

F32 = mybir.dt.float32
BF16 = mybir.dt.bfloat16
AF = mybir.ActivationFunctionType
ALU = mybir.AluOpType
AX = mybir.AxisListType

D = 1024
INW = 8704
EPS = 1e-6
NRING = 8
LAMBDA_INIT = 0.8 - 0.6 * math.exp(-0.3 * 0)
ATTACH_WAIT = True

FULL_CFG = dict(NP=2, SP=4096, SSK=8192, SSQ=2048)


class Ctx:
    def __init__(self, nc, es):
        self.nc = nc
        self.E = {"pe": nc.tensor, "act": nc.scalar, "dve": nc.vector, "pool": nc.gpsimd, "sp": nc.sync}
        self.sem = {}
        self.cnt = {}
        for n in self.E:
            self.sem[n] = es.enter_context(nc.semaphore("s_" + n))
            self.cnt[n] = 0
        self.seen = {n: {} for n in self.E}
        self.rings = {}
        for q in ("sp", "pool"):
            self.rings[q] = [[es.enter_context(nc.semaphore("d_%s%d" % (q, i))), 0] for i in range(NRING)]
        self.rpos = {q: 0 for q in self.rings}
        self.Tw = {}
        self.Tr = {}
        self.n_ins = 0

    def _waits(self, e, deps):
        best = {}
        for d in deps:
            if d is None:
                continue
            key, sem, val = d
            if key == e:
                if e == "pe" or self.cnt[e] - val >= 2:
                    continue
            if self.seen[e].get(key, 0) >= val:
                continue
            if key not in best or best[key][1] < val:
                best[key] = (sem, val)
        for key, (sem, val) in best.items():
            self.seen[e][key] = val
        return list(best.values())

    def _emit(self, e, fns, deps):
        eng = self.E[e]
        ws = self._waits(e, deps)
        if ATTACH_WAIT and ws:
            for sem, val in ws[:-1]:
                eng.wait_ge(sem, val)
                self.n_ins += 1
        else:
            for sem, val in ws:
                eng.wait_ge(sem, val)
                self.n_ins += 1
        ins = None
        for i, fn in enumerate(fns):
            ins = fn(eng)
            self.n_ins += 1
            if i == 0 and ATTACH_WAIT and ws:
                ins._wait_ge(*ws[-1])
        ins.then_inc(self.sem[e], 1)
        self.cnt[e] += 1
        return (e, self.sem[e], self.cnt[e])

    def _emit_dma(self, q, out, in_, deps):
        eng = self.E[q]
        idx = self.rpos[q] % NRING
        slot = self.rings[q][idx]
        self.rpos[q] += 1
        sem, val = slot
        key = "d_%s%d" % (q, idx)
        alld = list(deps)
        if val > 0:
            alld.append((key, sem, val))
        for s, v in self._waits(q, alld):
            eng.wait_ge(s, v)
            self.n_ins += 1
        eng.dma_start(out=out, in_=in_).then_inc(sem, 16)
        self.n_ins += 1
        slot[1] = val + 16
        return (key, sem, val + 16)

    def _deps(self, reads, writes, extra):
        deps = list(extra)
        for k in reads:
            deps += self.Tw.get(k, [])
        for k in writes:
            deps += self.Tw.get(k, [])
            deps += list(self.Tr.get(k, {}).values())
        return deps

    def _note(self, ev, reads, writes):
        for k in reads:
            d = self.Tr.setdefault(k, {})
            if ev[0] not in d or d[ev[0]][2] < ev[2]:
                d[ev[0]] = ev
        for k in writes:
            self.Tw[k] = [ev]
            self.Tr[k] = {}

    def run(self, e, fn, reads=(), writes=(), extra=()):
        fns = fn if isinstance(fn, (list, tuple)) else [fn]
        ev = self._emit(e, fns, self._deps(reads, writes, extra))
        self._note(ev, reads, writes)
        return ev

    def dma(self, q, out, in_, reads=(), writes=(), extra=()):
        ev = self._emit_dma(q, out, in_, self._deps(reads, writes, extra))
        self._note(ev, reads, writes)
        return ev

    def barrier(self):
        evs = [(n, self.sem[n], self.cnt[n]) for n in self.E if self.cnt[n] > 0]
        for q, ring in self.rings.items():
            for i, (sem, val) in enumerate(ring):
                if val > 0:
                    evs.append(("d_%s%d" % (q, i), sem, val))
        for e in self.E:
            for s, v in self._waits(e, evs):
                self.E[e].wait_ge(s, v)
                self.n_ins += 1
        self.Tw = {}
        self.Tr = {}


def _cap(ap, off, dims):
    return bass.AP(tensor=ap.tensor, offset=ap.offset + off, ap=[list(ap.ap[0])] + [list(d) for d in dims])


def _bcast_rows(dram_ap, off, n, parts=128):
    return bass.AP(tensor=dram_ap.tensor, offset=dram_ap.offset + off, ap=[[0, parts], [1, n]])


def _build(cfg):
    NP, SP, SSK, SSQ = cfg["NP"], cfg["SP"], cfg["SSK"], cfg["SSQ"]
    NOWN = SSQ // 128
    NO = SSK // 128 - NOWN
    SMK = max(SP, SSK)
    SMQ = max(SP, SSQ)
    nc = bass.Bass("TRN2", target_bir_lowering=False)
    dt = nc.dram_tensor

    def din(name, shape, dtype=F32):
        return dt(name, shape, dtype, kind="ExternalInput").ap()

    xp = din("xp", [NP, SP, D])
    xs = din("xs", [SSK, D])
    g_norm = din("g_norm", [1, D])
    g_final = din("g_final", [1, D])
    w_in = din("w_in", [D, INW])
    lamq = din("lamq", [4, 64])
    subln_w = din("subln_w", [1, 128])
    q_norm_b = din("q_norm_b", [1, 128])
    k_norm_b = din("k_norm_b", [1, 128])
    w_pa = din("w_pa", [D, D])
    w_pb = din("w_pb", [D, D])
    w_o = din("w_o", [D, D])
    rel_bias = din("rel_bias", [32, 8])
    ropeP = din("ropeP", [SP, 256])
    ropeS = din("ropeS", [SSK, 256])
    eye = din("eye", [128, 128])
    onehot = din("onehot", [32, 1280])
    onehot_n = din("onehot_n", [32, 640])
    onehot_p = din("onehot_p", [32, 640])
    side_in = din("side", [1, NO])
    yp = dt("yp", [NP, SP, D], F32, kind="ExternalOutput").ap()
    ys = dt("ys", [SSQ, D], F32, kind="ExternalOutput").ap()

    def dscr(name, shape, dtype):
        return dt(name, shape, dtype, kind="Internal").ap()

    QTa = dscr("QTa", [8, 128, SMQ], BF16)
    KTa = dscr("KTa", [8, 128, SMK], BF16)
    Va = dscr("Va", [SMK, 1024], BF16)
    QTb = dscr("QTb", [8, 128, SMQ], BF16)
    KTb = dscr("KTb", [2, 128, SMK], BF16)
    Vb = dscr("Vb", [SMK, 256], BF16)
    ZaT = dscr("ZaT", [8, 128, SMQ], BF16)
    ZbT = dscr("ZbT", [8, 128, SMQ], BF16)
    GaT = dscr("GaT", [8, 128, SMQ], BF16)
    GbT = dscr("GbT", [8, 128, SMQ], BF16)
    OaT = dscr("OaT", [8, 128, SMQ], F32)
    ObT = dscr("ObT", [8, 128, SMQ], F32)
    Wbf = dscr("Wbf", [3, 128, 8 * D], BF16)
    Tscr = dscr("Tscr", [8, 1280], F32)
    Tnscr = dscr("Tnscr", [8, 640], F32)
    Tpscr = dscr("Tpscr", [8, 640], F32)

    jobs = []
    for n in range(NP):
        jobs.append(dict(x=xp[n], Sk=SP, Sq=SP, rope=ropeP, out=yp[n], sample=False))
    jobs.append(dict(x=xs, Sk=SSK, Sq=SSQ, rope=ropeS, out=ys, sample=True))

    with ExitStack() as es:
        C = Ctx(nc, es)

        uid = [0]

        def SB(st, name, shape, dtype):
            uid[0] += 1
            return st.enter_context(nc.sbuf_tensor("%s_u%d" % (name, uid[0]), shape, dtype))

        def PS(st, name, shape, dtype):
            uid[0] += 1
            return st.enter_context(nc.psum_tensor("%s_u%d" % (name, uid[0]), shape, dtype))

        identb = SB(es, "identb", [128, 128], BF16)
        gn_bc = SB(es, "gn_bc", [128, D], F32)
        gf_bc = SB(es, "gf_bc", [128, D], F32)
        gq_bc = SB(es, "gq_bc", [128, 128], F32)
        gk_bc = SB(es, "gk_bc", [128, 128], F32)
        sublnc = SB(es, "sublnc", [128, 1], F32)
        ggs_q = SB(es, "ggs_q", [128, 256], F32)
        ggs_k = SB(es, "ggs_k", [128, 256], F32)
        neglam = SB(es, "neglam", [128, 1], F32)
        SelA = SB(es, "SelA", [128, 128], F32)
        SelBb = SB(es, "SelBb", [128, 128], F32)
        ones32 = SB(es, "ones32", [128, 32], BF16)
        onesf = SB(es, "onesf", [128, 128], F32)
        clo = SB(es, "clo", [128, 8], F32)
        chi = SB(es, "chi", [128, 8], F32)
        fb = SB(es, "fb", [128, 8, NO], F32)
        epsc = SB(es, "epsc", [128, 1], F32)
        zeroc = SB(es, "zeroc", [128, 1], F32)
        onec = SB(es, "onec", [128, 1], F32)

        with ExitStack() as st:
            identf = SB(st, "identf", [128, 128], F32)
            subl = SB(st, "subl", [128, 1], F32)
            lamv = SB(st, "lamv", [128, 4, 64], F32)
            lprod = SB(st, "lprod", [128, 2, 64], F32)
            lred = SB(st, "lred", [128, 2], F32)
            lexp = SB(st, "lexp", [128, 2], F32)
            rbs = SB(st, "rbs", [32, 8], F32)
            oh = SB(st, "oh", [32, 1280], F32)
            ohn = SB(st, "ohn", [32, 640], F32)
            ohp = SB(st, "ohp", [32, 640], F32)
            side = SB(st, "side", [128, NO], F32)
            dif = SB(st, "dif", [128, 8], F32)
            Tsb = SB(st, "Tsb", [8, 1280], F32)
            Tnsb = SB(st, "Tnsb", [8, 640], F32)
            Tpsb = SB(st, "Tpsb", [8, 640], F32)
            pz = PS(st, "pz", [128, 512], F32)

            C.dma("sp", identf[:], eye[:, :], writes=["identf"])
            C.dma("sp", gn_bc[:], _bcast_rows(g_norm, 0, D), writes=["gn_bc"])
            C.dma("sp", gf_bc[:], _bcast_rows(g_final, 0, D), writes=["gf_bc"])
            C.dma("sp", gq_bc[:], _bcast_rows(q_norm_b, 0, 128), writes=["gq_bc"])
            C.dma("sp", gk_bc[:], _bcast_rows(k_norm_b, 0, 128), writes=["gk_bc"])
            C.dma("sp", subl[:], bass.AP(tensor=subln_w.tensor, offset=subln_w.offset, ap=[[1, 128], [1, 1]]),
                  writes=["subl"])
            for i in range(4):
                C.dma("sp", lamv[:, i, :], _bcast_rows(lamq, 64 * i, 64), writes=[("lamv", i)])
            C.dma("sp", rbs[:], rel_bias[:, :], writes=["rbs"])
            C.dma("sp", oh[:], onehot[:, :], writes=["oh"])
            C.dma("sp", ohn[:], onehot_n[:, :], writes=["ohn"])
            C.dma("sp", ohp[:], onehot_p[:, :], writes=["ohp"])
            C.dma("sp", clo[:], _bcast_rows(rel_bias, 15 * 8, 8), writes=["clo"])
            C.dma("sp", chi[:], _bcast_rows(rel_bias, 31 * 8, 8), writes=["chi"])
            C.dma("sp", side[:], _bcast_rows(side_in, 0, NO), writes=["side"])

            C.run("dve", lambda e: e.tensor_copy(out=identb[:], in_=identf[:]), reads=["identf"], writes=["identb"])
            for (gsrc, gk, gdst, gdk) in ((gq_bc, "gq_bc", ggs_q, "ggs_q"), (gk_bc, "gk_bc", ggs_k, "ggs_k")):
                C.run("dve", lambda e: e.tensor_copy(out=gdst[:, 0:128], in_=gsrc[:]), reads=[gk], writes=[(gdk, 0)])
                for q4 in range(4):
                    src_off = (q4 ^ 1) * 32
                    C.run("dve", lambda e: e.tensor_copy(out=gdst[:, 128 + q4 * 32:128 + (q4 + 1) * 32],
                                                         in_=gsrc[:, src_off:src_off + 32]),
                          reads=[gk], writes=[(gdk, 1 + q4)])
            C.run("dve", lambda e: e.tensor_scalar(out=sublnc[:], in0=subl[:], scalar1=1.0 - LAMBDA_INIT,
                                                   scalar2=None, op0=ALU.mult), reads=["subl"], writes=["sublnc"])
            for i in range(2):
                C.run("dve", lambda e: e.tensor_tensor(out=lprod[:, i, :], in0=lamv[:, 2 * i, :],
                                                       in1=lamv[:, 2 * i + 1, :], op=ALU.mult),
                      reads=[("lamv", 2 * i), ("lamv", 2 * i + 1)], writes=[("lprod", i)])
                C.run("dve", lambda e: e.tensor_reduce(out=lred[:, i:i + 1], in_=lprod[:, i, :], axis=AX.X, op=ALU.add),
                      reads=[("lprod", i)], writes=[("lred", i)])
            C.run("act", lambda e: e.activation(out=lexp[:], in_=lred[:], func=AF.Exp),
                  reads=[("lred", 0), ("lred", 1)], writes=["lexp"])
            C.run("dve", lambda e: e.tensor_tensor(out=neglam[:], in0=lexp[:, 1:2], in1=lexp[:, 0:1], op=ALU.subtract),
                  reads=["lexp"], writes=["neglam"])
            C.run("dve", lambda e: e.tensor_scalar(out=neglam[:], in0=neglam[:], scalar1=-LAMBDA_INIT, scalar2=None,
                                                   op0=ALU.add), reads=["neglam"], writes=["neglam"])
            C.run("dve", lambda e: e.memset(SelA[:], 0.0), writes=["SelA"])
            C.run("dve", lambda e: e.memset(SelA[0:32, :], 1.0 / 32), writes=["SelA"])
            C.run("dve", lambda e: e.memset(SelA[64:96, :], 1.0 / 32), writes=["SelA"])
            C.run("dve", lambda e: e.memset(SelBb[:], 1.0 / 32), writes=["SelBb"])
            C.run("dve", lambda e: e.memset(SelBb[0:32, :], 0.0), writes=["SelBb"])
            C.run("dve", lambda e: e.memset(SelBb[64:96, :], 0.0), writes=["SelBb"])
            C.run("dve", lambda e: e.memset(ones32[:], 1.0), writes=["ones32"])
            C.run("dve", lambda e: e.memset(onesf[:], 1.0 / 128), writes=["onesf"])
            C.run("dve", lambda e: e.memset(epsc[:], EPS), writes=["epsc"])
            C.run("dve", lambda e: e.memset(zeroc[:], 0.0), writes=["zeroc"])
            C.run("dve", lambda e: e.memset(onec[:], 1.0), writes=["onec"])
            C.run("dve", lambda e: e.tensor_tensor(out=dif[:], in0=chi[:], in1=clo[:], op=ALU.subtract),
                  reads=["chi", "clo"], writes=["dif"])
            for h in range(8):
                C.run("dve", lambda e: e.tensor_scalar(out=fb[:, h, :], in0=side[:], scalar1=dif[:, h:h + 1],
                                                       scalar2=clo[:, h:h + 1], op0=ALU.mult, op1=ALU.add),
                      reads=["side", "dif", "clo"], writes=[("fb", h)])
            wstg = [SB(st, "wstg%d" % i, [128, 8, 512], F32) for i in range(2)]
            wstb = [SB(st, "wstb%d" % i, [128, 8, 512], BF16) for i in range(2)]
            wc = 0
            for wi_, wsrc in enumerate((w_pa, w_pb, w_o)):
                for i in range(2):
                    x_ = wc % 2
                    wc += 1
                    C.dma("sp", wstg[x_][:], wsrc[:, 512 * i:512 * (i + 1)].rearrange("(k p) c -> p k c", p=128),
                          writes=[("wstg", x_)])
                    C.run("pool", lambda e: e.tensor_copy(out=wstb[x_][:], in_=wstg[x_][:]),
                          reads=[("wstg", x_)], writes=[("wstb", x_)])
                    C.dma("pool", _cap(Wbf[wi_], 512 * i, [[D, 8], [1, 512]]), wstb[x_][:], reads=[("wstb", x_)])
            for (src, srck, dst, dstk, width, scr) in ((oh, "oh", Tsb, "Tsb", 1280, Tscr),
                                                       (ohn, "ohn", Tnsb, "Tnsb", 640, Tnscr),
                                                       (ohp, "ohp", Tpsb, "Tpsb", 640, Tpscr)):
                c0 = 0
                while c0 < width:
                    n = min(512, width - c0)
                    C.run("pe", lambda e: e.matmul(pz[0:8, 0:n], lhsT=rbs[0:32, 0:8], rhs=src[0:32, c0:c0 + n],
                                                   start=True, stop=True), reads=["rbs", srck], writes=["pz"])
                    C.run("dve", lambda e: e.tensor_scalar(out=dst[0:8, c0:c0 + n], in0=pz[0:8, 0:n], scalar1=8.0,
                                                           scalar2=None, op0=ALU.mult), reads=["pz"], writes=[dstk])
                    c0 += n
                C.dma("pool", scr[:, :], dst[0:8, :], reads=[dstk])
        C.barrier()

        for jb, job in enumerate(jobs):
            xkv, Sk, Sq, rope, outp, sample = job["x"], job["Sk"], job["Sq"], job["rope"], job["out"], job["sample"]
            nqt = Sq // 512
            nkb = Sk // 128

            with ExitStack() as st:
                SUB = min(Sk, 4096)
                xnT = SB(st, "xnT", [128, 8, SUB], BF16)
                xtb = [SB(st, "xt%d" % i, [128, 4, D], F32) for i in range(2)]
                junk = SB(st, "junk", [128, D], BF16)
                ssq = SB(st, "ssq", [128, 8], F32)
                lnt = SB(st, "lnt", [128, 8], F32)
                rstd = SB(st, "rstd", [128, 8], F32)
                xnb = [SB(st, "xn%d" % i, [128, D], BF16) for i in range(2)]
                wst = SB(st, "wst", [128, 8, 512], F32)
                wbb = [SB(st, "wb%d" % i, [128, 8, 512], BF16) for i in range(2)]
                stg = [SB(st, "stg%d" % i, [128, 4, 512], BF16) for i in range(2)]
                rpb = [SB(st, "rp%d" % i, [128, 4, 256], F32) for i in range(2)]
                qn_r = [SB(st, "qn%d" % i, [128, 512], F32) for i in range(2)]
                qo1_r = [SB(st, "qo1%d" % i, [128, 512], F32) for i in range(2)]
                qtm_r = [SB(st, "qtm%d" % i, [128, 512], F32) for i in range(2)]
                qrot_r = [SB(st, "qrot%d" % i, [128, 512], BF16) for i in range(2)]
                qjunk = SB(st, "qjunk", [128, 512], BF16)
                qss_r = [SB(st, "qss%d" % i, [128, 4], F32) for i in range(2)]
                qln_r = [SB(st, "qln%d" % i, [128, 4], F32) for i in range(2)]
                qrs_r = [SB(st, "qrs%d" % i, [128, 4], F32) for i in range(2)]
                pm = [PS(st, "pm%d" % i, [128, 512], F32) for i in range(4)]
                tp = [PS(st, "tp%d" % i, [128, 8, 128], BF16) for i in range(2)]
                tq = [PS(st, "tq%d" % i, [128, 8, 128], BF16) for i in range(2)]
                cnt = dict(pm=0, ev=0, stg=0, wb=0, tq=0, rp=0, rb=0)

                def evac(out_ap, in_ap, reads, writes):
                    cnt["ev"] += 1
                    if cnt["ev"] % 2 == 0:
                        return C.run("dve", lambda e: e.tensor_copy(out=out_ap, in_=in_ap), reads=reads, writes=writes)
                    return C.run("act", lambda e: e.activation(out=out_ap, in_=in_ap, func=AF.Copy),
                                 reads=reads, writes=writes)

                def rope_piece(wb, wi, ncol, dest, doff, ntiles, t0):
                    nh = ncol // 128
                    ggs = ggs_q if dest is QTb else ggs_k
                    blocks = [(tt, b) for tt in range(ntiles) for b in range(4)]
                    info = {}

                    def mm(idx):
                        tt, b = blocks[idx]
                        if b == 0:
                            ri = cnt["rp"] % 2
                            cnt["rp"] += 1
                            tok = t0 + tt * 512
                            C.dma("sp", rpb[ri][:], rope[tok:tok + 512, :].rearrange("(b p) c -> p b c", p=128),
                                  writes=[("rp", ri)])
                            C.run("pool", lambda e: e.tensor_tensor(out=rpb[ri][:], in0=rpb[ri][:],
                                                                    in1=_cap(ggs[:], 0, [[0, 4], [1, 256]]), op=ALU.mult),
                                  reads=[("rp", ri)], writes=[("rp", ri)])
                            si = cnt["stg"] % 2
                            cnt["stg"] += 1
                            info[tt] = (ri, si)
                        pi = cnt["pm"] % 4
                        cnt["pm"] += 1
                        C.run("pe", [(lambda e, k=k: e.matmul(pm[pi][:, 0:ncol],
                                                               lhsT=xnT[:, k, tt * 512 + b * 128: tt * 512 + (b + 1) * 128],
                                                               rhs=wb[:, k, 0:ncol], start=(k == 0), stop=(k == 7)))
                                     for k in range(8)],
                              reads=[("wb", wi), ("xnT", tt)], writes=[("pm", pi)])
                        return pi

                    pis = {0: mm(0)}
                    if len(blocks) > 1:
                        pis[1] = mm(1)
                    pend = []
                    for idx, (tt, b) in enumerate(blocks):
                        if idx + 2 < len(blocks):
                            pis[idx + 2] = mm(idx + 2)
                        pi = pis.pop(idx)
                        ri, si = info[tt]
                        rp = rpb[ri]
                        tok = t0 + tt * 512
                        rb_ = cnt["rb"] % 2
                        cnt["rb"] += 1
                        qn, qo1, qtm, qrot = qn_r[rb_], qo1_r[rb_], qtm_r[rb_], qrot_r[rb_]
                        qss, qln, qrs = qss_r[rb_], qln_r[rb_], qrs_r[rb_]
                        K_ = lambda nm, *a: (nm, rb_) + a
                        for h in range(nh):
                            C.run("act", lambda e: e.activation(out=qjunk[:, h * 128:(h + 1) * 128],
                                                                in_=pm[pi][:, h * 128:(h + 1) * 128],
                                                                func=AF.Square, accum_out=qss[:, h:h + 1]),
                                  reads=[("pm", pi)], writes=[("qjunk", h), K_("qss", h)])
                        C.run("act", lambda e: e.activation(out=qln[:, 0:nh], in_=qss[:, 0:nh], func=AF.Ln,
                                                            bias=epsc[:], scale=1.0 / 128),
                              reads=[K_("qss", h) for h in range(nh)], writes=[K_("qln")])
                        C.run("act", lambda e: e.activation(out=qrs[:, 0:nh], in_=qln[:, 0:nh], func=AF.Exp,
                                                            scale=-0.5), reads=[K_("qln")], writes=[K_("qrs")])
                        for h in range(0, nh, 2):
                            C.run("act", lambda e: e.activation(out=qn[:, h * 128:(h + 1) * 128],
                                                                in_=pm[pi][:, h * 128:(h + 1) * 128], func=AF.Copy,
                                                                scale=qrs[:, h:h + 1]),
                                  reads=[("pm", pi), K_("qrs")], writes=[K_("qn", h), K_("pmtok")])
                        for h in range(1, nh, 2):
                            C.run("dve", lambda e: e.tensor_scalar(out=qn[:, h * 128:(h + 1) * 128],
                                                                   in0=pm[pi][:, h * 128:(h + 1) * 128],
                                                                   scalar1=qrs[:, h:h + 1], scalar2=None, op0=ALU.mult),
                                  reads=[("pm", pi), K_("qrs"), K_("pmtok")], writes=[K_("qn", h)])
                        qnk = [K_("qn", h) for h in range(nh)]
                        qn_a = qn[:, 0:ncol]
                        rp_a = rp[:, b, :]
                        C.run("pool", lambda e: e.tensor_tensor(
                            out=_cap(qo1[:, 0:ncol], 0, [[128, nh], [1, 128]]),
                            in0=_cap(qn_a, 0, [[128, nh], [1, 128]]),
                            in1=_cap(rp_a, 0, [[0, nh], [1, 128]]), op=ALU.mult),
                              reads=qnk + [("rp", ri)], writes=[K_("qo1")])
                        C.run("dve", lambda e: e.tensor_tensor(
                            out=_cap(qtm[:, 0:ncol], 0, [[128, nh], [64, 2], [1, 32]]),
                            in0=_cap(qn_a, 32, [[128, nh], [64, 2], [1, 32]]),
                            in1=_cap(rp_a, 128, [[0, nh], [64, 2], [1, 32]]), op=ALU.mult),
                              reads=qnk + [("rp", ri)], writes=[K_("qtm", 0)])
                        C.run("dve", lambda e: e.tensor_tensor(
                            out=_cap(qtm[:, 0:ncol], 32, [[128, nh], [64, 2], [1, 32]]),
                            in0=_cap(qn_a, 0, [[128, nh], [64, 2], [1, 32]]),
                            in1=_cap(rp_a, 128 + 32, [[0, nh], [64, 2], [1, 32]]), op=ALU.mult),
                              reads=qnk + [("rp", ri)], writes=[K_("qtm", 1)])
                        C.run("pool", lambda e: e.tensor_tensor(out=qrot[:, 0:ncol], in0=qo1[:, 0:ncol],
                                                                in1=qtm[:, 0:ncol], op=ALU.add),
                              reads=[K_("qo1"), K_("qtm", 0), K_("qtm", 1)], writes=[K_("qrot")])
                        ti = cnt["tq"] % 2
                        cnt["tq"] += 1
                        C.run("pe", [(lambda e, h=h: e.transpose(tq[ti][:, h, :], qrot[:, h * 128:(h + 1) * 128],
                                                                  identb[:])) for h in range(nh)],
                              reads=[K_("qrot")], writes=[("tq", ti)])
                        def fin(si=si, b=b, ti=ti, tok=tok):
                            evac(stg[si][:, 0:nh, b * 128:(b + 1) * 128], tq[ti][:, 0:nh, :],
                                 reads=[("tq", ti)], writes=[("stg", si, b)])
                            if b == 3:
                                C.dma("pool", dest[doff:doff + nh, :, tok:tok + 512].rearrange("c p t -> p c t"),
                                      stg[si][:, 0:nh, :], reads=[("stg", si, bb) for bb in range(4)])

                        if pend:
                            pend.pop()()
                        pend.append(fin)
                    while pend:
                        pend.pop()()

                for t0 in range(0, Sk, SUB):
                    ntt = SUB // 512
                    nqtt = max(0, min(ntt, (Sq - t0) // 512))
                    for tt in range(ntt):
                        xt = xtb[tt % 2]
                        xk = ("xt", tt % 2)
                        xsrc = xkv[t0 + tt * 512:t0 + (tt + 1) * 512, :].rearrange("(b p) c -> p b c", p=128)
                        C.dma("sp", xt[:, 0:2, :], xsrc[:, 0:2, :], writes=[xk + (0,)])
                        C.dma("pool", xt[:, 2:4, :], xsrc[:, 2:4, :], writes=[xk + (1,)])
                        for b in range(4):
                            blk = tt * 4 + b
                            col = blk % 8
                            C.run("act", lambda e: e.activation(out=junk[:], in_=xt[:, b, :], func=AF.Square,
                                                                accum_out=ssq[:, col:col + 1]),
                                  reads=[xk + (b // 2,)], writes=["junk", ("ssq", col)])
                            C.run("act", lambda e: e.activation(out=lnt[:, col:col + 1], in_=ssq[:, col:col + 1],
                                                                func=AF.Ln, bias=epsc[:], scale=1.0 / D),
                                  reads=[("ssq", col)], writes=[("lnt", col)])
                            C.run("act", lambda e: e.activation(out=rstd[:, col:col + 1], in_=lnt[:, col:col + 1],
                                                                func=AF.Exp, scale=-0.5),
                                  reads=[("lnt", col)], writes=[("rstd", col)])
                            xn = xnb[blk % 2]
                            C.run("dve", lambda e: e.scalar_tensor_tensor(out=xn[:], in0=xt[:, b, :],
                                                                          scalar=rstd[:, col:col + 1], in1=gn_bc[:],
                                                                          op0=ALU.mult, op1=ALU.mult),
                                  reads=[xk + (b // 2,), ("rstd", col)], writes=[("xn", blk % 2)])
                            tpp = tp[blk % 2]
                            C.run("pe", [(lambda e, c=c: e.transpose(tpp[:, c, :], xn[:, c * 128:(c + 1) * 128], identb[:]))
                                         for c in range(8)], reads=[("xn", blk % 2)], writes=[("tp", blk % 2)])
                            evac(xnT[:, :, tt * 512 + b * 128: tt * 512 + (b + 1) * 128], tpp[:, :, :],
                                 reads=[("tp", blk % 2)], writes=[("xnT", tt)])

                    pieces = []
                    for i in range(2):
                        pieces.append(("fm", 0 + 512 * i, 512, QTa, 4 * i, True))
                    for i in range(2):
                        pieces.append(("fm", 1024 + 512 * i, 512, KTa, 4 * i, False))
                    for i in range(2):
                        pieces.append(("tm", 2048 + 512 * i, 512, Va, 512 * i, False))
                    for i in range(2):
                        pieces.append(("fm", 3072 + 512 * i, 512, ZaT, 4 * i, True))
                    for i in range(2):
                        pieces.append(("rope", 4096 + 512 * i, 512, QTb, 4 * i, True))
                    pieces.append(("rope", 5120, 256, KTb, 0, False))
                    pieces.append(("tm", 5376, 256, Vb, 0, False))
                    for i in range(2):
                        pieces.append(("fm", 5632 + 512 * i, 512, ZbT, 4 * i, True))
                    for i in range(2):
                        pieces.append(("fm", 6656 + 512 * i, 512, GaT, 4 * i, True))
                    for i in range(2):
                        pieces.append(("fm", 7680 + 512 * i, 512, GbT, 4 * i, True))

                    active = [p for p in pieces if (nqtt if p[5] else ntt) > 0]

                    def load_w(pi_):
                        if pi_ >= len(active):
                            return
                        _, c0_, nc_, _, _, _ = active[pi_]
                        C.dma("sp", wst[:, :, 0:nc_], w_in[:, c0_:c0_ + nc_].rearrange("(k p) c -> p k c", p=128),
                              writes=["wst"])
                        C.run("pool", lambda e: e.tensor_copy(out=wbb[pi_ % 2][:, :, 0:nc_], in_=wst[:, :, 0:nc_]),
                              reads=["wst"], writes=[("wb", pi_ % 2)])

                    load_w(0)
                    for pidx, (kind, col0, ncol, dest, doff, qside) in enumerate(active):
                        ntiles = nqtt if qside else ntt
                        wi = pidx % 2
                        wb = wbb[wi]
                        load_w(pidx + 1)
                        if kind == "rope":
                            rope_piece(wb, wi, ncol, dest, doff, ntiles, t0)
                            continue
                        for tt in range(ntiles):
                            tok = t0 + tt * 512
                            if kind == "fm":
                                si = cnt["stg"] % 2
                                cnt["stg"] += 1
                                for s in range(4):
                                    pi = cnt["pm"] % 4
                                    cnt["pm"] += 1
                                    C.run("pe", [(lambda e, k=k: e.matmul(pm[pi][:], lhsT=wb[:, k, s * 128:(s + 1) * 128],
                                                                           rhs=xnT[:, k, tt * 512:(tt + 1) * 512],
                                                                           start=(k == 0), stop=(k == 7)))
                                                 for k in range(8)],
                                          reads=[("wb", wi), ("xnT", tt)], writes=[("pm", pi)])
                                    evac(stg[si][:, s, :], pm[pi][:], reads=[("pm", pi)], writes=[("stg", si, s)])
                                C.dma("pool", dest[doff:doff + 4, :, tok:tok + 512].rearrange("c p t -> p c t"),
                                      stg[si][:], reads=[("stg", si, s) for s in range(4)])
                            elif kind == "tm":
                                si = cnt["stg"] % 2
                                cnt["stg"] += 1
                                for b in range(4):
                                    pi = cnt["pm"] % 4
                                    cnt["pm"] += 1
                                    C.run("pe", [(lambda e, k=k: e.matmul(pm[pi][:, 0:ncol],
                                                                           lhsT=xnT[:, k, tt * 512 + b * 128: tt * 512 + (b + 1) * 128],
                                                                           rhs=wb[:, k, 0:ncol],
                                                                           start=(k == 0), stop=(k == 7)))
                                                 for k in range(8)],
                                          reads=[("wb", wi), ("xnT", tt)], writes=[("pm", pi)])
                                    evac(stg[si][:, b, 0:ncol], pm[pi][:, 0:ncol], reads=[("pm", pi)],
                                         writes=[("stg", si, b)])
                                C.dma("pool", dest[tok:tok + 512, doff:doff + ncol].rearrange("(b p) c -> p b c", p=128),
                                      stg[si][:, :, 0:ncol], reads=[("stg", si, b) for b in range(4)])
            C.barrier()

            jw = ExitStack()
            Wpa = SB(jw, "Wpa", [128, 8, D], BF16)
            Wpb = SB(jw, "Wpb", [128, 8, D], BF16)
            Wo = SB(jw, "Wo", [128, 8, D], BF16)

            with ExitStack() as st:
                QTt = [[SB(st, "QT%d_%d" % (i, m), [128, Sq], BF16) for m in range(2)] for i in range(2)]
                KTt = [SB(st, "KT%d" % i, [128, Sk], BF16) for i in range(2)]
                Vt = [SB(st, "V%d" % i, [128, nkb, 128], BF16) for i in range(2)]
                Grt = [SB(st, "Gr%d" % i, [128, 1152], F32) for i in range(2)]
                Gnt = [SB(st, "Gn%d" % i, [128, 512], F32) for i in range(2)]
                Gpt = [SB(st, "Gp%d" % i, [128, 512], F32) for i in range(2)]
                NE = 6
                ebuf = [SB(st, "e%d" % i, [128, 2, 512], BF16) for i in range(NE)]
                tmpb = [SB(st, "tmpb%d" % i, [128, 2, 512], F32) for i in range(2)]
                Osb = [SB(st, "Osb%d" % i, [128, 2, 512], F32) for i in range(2)]
                ssb = [SB(st, "ssb%d" % i, [128, 512], F32) for i in range(2)]
                rsb = [SB(st, "rsb%d" % i, [128, 512], F32) for i in range(2)]
                ob = [SB(st, "ob%d" % i, [128, 512], F32) for i in range(2)]
                ob2 = [SB(st, "ob2_%d" % i, [128, 512], F32) for i in range(2)]
                sc = [PS(st, "sc%d" % i, [128, 2, 512], F32) for i in range(2)]
                Ops = PS(st, "Ops", [128, 2, 512], F32)
                smp = PS(st, "smp", [128, 512], F32)
                bcp = PS(st, "bcp", [128, 512], F32)

                hus = [("a", h) for h in range(8)] + [("b", n, pr) for n in range(2) for pr in range(2)]
                units = []
                for hi, hu in enumerate(hus):
                    for i in range(nqt):
                        for j in range(nkb):
                            units.append((hi, i, j))
                loaded = {}

                def load_hu(hi):
                    if hi in loaded or hi >= len(hus):
                        return
                    loaded[hi] = True
                    hu = hus[hi]
                    s = hi % 2
                    if hu[0] == "a":
                        h = hu[1]
                        C.dma("sp", QTt[s][0][:], QTa[h, :, 0:Sq], writes=[("QT", s, 0)])
                        C.dma("sp", KTt[s][:], KTa[h, :, 0:Sk], writes=[("KT", s)])
                        C.dma("sp", Vt[s][:], Va[0:Sk, h * 128:(h + 1) * 128].rearrange("(j p) d -> p j d", p=128),
                              writes=[("V", s)])
                        C.dma("sp", Grt[s][:], bass.AP(tensor=Tscr.tensor, offset=Tscr.offset + h * 1280,
                                                      ap=[[1, 128], [1, 1152]]), writes=[("Gr", s)])
                        if sample:
                            C.dma("sp", Gnt[s][:], bass.AP(tensor=Tnscr.tensor, offset=Tnscr.offset + h * 640,
                                                          ap=[[1, 128], [1, 512]]), writes=[("Gn", s)])
                            C.dma("sp", Gpt[s][:], bass.AP(tensor=Tpscr.tensor, offset=Tpscr.offset + h * 640,
                                                          ap=[[1, 128], [1, 512]]), writes=[("Gp", s)])
                    else:
                        n, pr = hu[1], hu[2]
                        g0 = n * 4 + 2 * pr
                        C.dma("sp", QTt[s][0][:], QTb[g0, :, 0:Sq], writes=[("QT", s, 0)])
                        C.dma("sp", QTt[s][1][:], QTb[g0 + 1, :, 0:Sq], writes=[("QT", s, 1)])
                        C.dma("sp", KTt[s][:], KTb[n, :, 0:Sk], writes=[("KT", s)])
                        C.dma("sp", Vt[s][:], Vb[0:Sk, n * 128:(n + 1) * 128].rearrange("(j p) d -> p j d", p=128),
                              writes=[("V", s)])

                def bias_mode(hu, s, i, j):
                    if hu[0] == "b":
                        return ("far", zeroc[:, 0:1], [])
                    h = hu[1]
                    if (not sample) or j < NOWN:
                        d = j - 4 * i
                        if -1 <= d <= 4:
                            return ("near", _cap(Grt[s][:], 639 + 128 * d, [[0, 2], [-1, 512]]), [("Gr", s)])
                        return ("far", (clo if d < -1 else chi)[:, h:h + 1], [])
                    if j == NOWN and i == nqt - 1:
                        return ("near", _cap(Gnt[s][:], 511, [[0, 2], [-1, 512]]), [("Gn", s)])
                    if j == nkb - 1 and i == 0:
                        return ("near", _cap(Gpt[s][:], 511, [[0, 2], [-1, 512]]), [("Gp", s)])
                    return ("far", fb[:, h, j - NOWN:j - NOWN + 1], [])

                def emit_qk(u):
                    hi, i, j = units[u]
                    hu = hus[hi]
                    s = hi % 2
                    scp = sc[u % 2]
                    q0 = i * 512
                    if hu[0] == "a":
                        fns = [lambda e: e.matmul(scp[:, 0, :], lhsT=KTt[s][0:64, j * 128:(j + 1) * 128],
                                                  rhs=QTt[s][0][0:64, q0:q0 + 512], start=True, stop=True,
                                                  tile_position=(0, 0)),
                               lambda e: e.matmul(scp[:, 1, :], lhsT=KTt[s][64:128, j * 128:(j + 1) * 128],
                                                  rhs=QTt[s][0][64:128, q0:q0 + 512], start=True, stop=True,
                                                  tile_position=(64, 0))]
                        rd = [("QT", s, 0), ("KT", s)]
                    else:
                        fns = [lambda e: e.matmul(scp[:, 0, :], lhsT=KTt[s][:, j * 128:(j + 1) * 128],
                                                  rhs=QTt[s][0][:, q0:q0 + 512], start=True, stop=True),
                               lambda e: e.matmul(scp[:, 1, :], lhsT=KTt[s][:, j * 128:(j + 1) * 128],
                                                  rhs=QTt[s][1][:, q0:q0 + 512], start=True, stop=True)]
                        rd = [("QT", s, 0), ("QT", s, 1), ("KT", s)]
                    C.run("pe", fns, reads=rd, writes=[("sc", u % 2)])

                deferred = {}

                def emit_epilogue(u, tcount):
                    hi, i, j = units[u]
                    hu = hus[hi]
                    x = tcount % 2
                    q0 = i * 512
                    C.run("dve", lambda e: e.tensor_copy(out=Osb[x][:], in_=Ops[:]), reads=["Ops"], writes=[("Osb", x)])
                    C.run("dve", lambda e: e.tensor_copy(out=ssb[x][:], in_=smp[:]), reads=["smp"],
                          writes=[("ssb", x)])
                    isa = hu[0] == "a"

                    def part1():
                        C.run("pe", lambda e: e.matmul(bcp[:], lhsT=SelA[:], rhs=ssb[x][:], start=True, stop=True),
                              reads=[("ssb", x)], writes=["bcp"])
                        C.run("dve", lambda e: e.reciprocal(out=rsb[x][:], in_=bcp[:]), reads=["bcp"], writes=[("rsb", x)])
                        C.run("dve", lambda e: e.scalar_tensor_tensor(out=ob[x][:], in0=Osb[x][:, 0, :], scalar=1.0,
                                                                      in1=rsb[x][:], op0=ALU.mult, op1=ALU.mult),
                              reads=[("Osb", x), ("rsb", x)], writes=[("ob", x)])

                    def part2():
                        C.run("pe", lambda e: e.matmul(bcp[:], lhsT=SelBb[:], rhs=ssb[x][:], start=True, stop=True),
                              reads=[("ssb", x)], writes=["bcp"])
                        C.run("dve", lambda e: e.reciprocal(out=rsb[x][:], in_=bcp[:]), reads=["bcp"], writes=[("rsb", x)])
                        C.run("dve", lambda e: e.scalar_tensor_tensor(out=ob2[x][:], in0=Osb[x][:, 1, :], scalar=1.0,
                                                                      in1=rsb[x][:], op0=ALU.mult, op1=ALU.mult),
                              reads=[("Osb", x), ("rsb", x)], writes=[("ob2", x)])
                        if isa:
                            C.run("dve", lambda e: e.scalar_tensor_tensor(out=ob[x][:], in0=ob2[x][:], scalar=neglam[:, 0:1],
                                                                          in1=ob[x][:], op0=ALU.mult, op1=ALU.add),
                                  reads=[("ob", x), ("ob2", x)], writes=[("ob", x)])
                            C.dma("pool", OaT[hu[1], :, q0:q0 + 512], ob[x][:], reads=[("ob", x)])
                        else:
                            g0 = hu[1] * 4 + 2 * hu[2]
                            C.dma("pool", ObT[g0, :, q0:q0 + 512], ob[x][:], reads=[("ob", x)])
                            C.dma("pool", ObT[g0 + 1, :, q0:q0 + 512], ob2[x][:], reads=[("ob2", x)])

                    p1, p2 = 2, 4
                    if nkb >= 24 and u + 1 < NU:
                        hi2, i2, _ = units[u + 1]
                        if hus[hi2][0] == "b" or 4 * i2 - 1 >= 14:
                            p1, p2 = 2, 6
                        else:
                            p1, p2 = 4 * i2 + 7, 4 * i2 + 11
                    deferred.setdefault(u + 1 + p1, []).append(part1)
                    deferred.setdefault(u + 1 + p2, []).append(part2)

                NU = len(units)
                load_hu(0)
                load_hu(1)
                emit_qk(0)
                if NU > 1:
                    emit_qk(1)
                for wi_, wdst in enumerate((Wpa, Wpb, Wo)):
                    C.dma("sp", wdst[:], Wbf[wi_].rearrange("p (k c) -> p k c", k=8))
                tcount = 0
                for u in range(NU):
                    hi, i, j = units[u]
                    hu = hus[hi]
                    s = hi % 2
                    if i == 0 and j == 0 and u > 0:
                        pass
                    mode, bap, brd = bias_mode(hu, s, i, j)
                    scale = 0.125 if hu[0] == "a" else 128 ** -0.5
                    eb = ebuf[u % NE]
                    if mode == "near":
                        tb = tmpb[u % 2]
                        C.run("dve", lambda e: e.tensor_tensor(out=tb[:], in0=sc[u % 2][:], in1=bap, op=ALU.add),
                              reads=[("sc", u % 2)] + brd, writes=[("tmpb", u % 2)])
                        C.run("act", lambda e: e.activation(out=eb[:], in_=tb[:], func=AF.Exp, bias=zeroc[:, 0:1],
                                                            scale=scale),
                              reads=[("tmpb", u % 2)], writes=[("e", u % NE)])
                    else:
                        C.run("act", lambda e: e.activation(out=eb[:], in_=sc[u % 2][:], func=AF.Exp, bias=bap, scale=scale),
                              reads=[("sc", u % 2)], writes=[("e", u % NE)])
                    first = j == 0
                    last = j == nkb - 1
                    if u + 2 < NU:
                        emit_qk(u + 2)
                    fns = [lambda e: e.matmul(Ops[:, 0, :], lhsT=Vt[s][:, j, :], rhs=eb[:, 0, :], start=first, stop=last),
                           lambda e: e.matmul(Ops[:, 1, :], lhsT=Vt[s][:, j, :], rhs=eb[:, 1, :], start=first, stop=last)]
                    rds = [("e", u % NE), ("V", s)]
                    wrs = ["Ops"]
                    if j % 4 == 3:
                        for back in (3, 2, 1, 0):
                            eb_ = ebuf[(u - back) % NE]
                            qb_ = 0 if back in (3, 1) else 2
                            st_ = (j == 3) and back >= 2
                            sp_ = last and back <= 1
                            for m in range(2):
                                fns.append(lambda e, qd=qb_ + m, src=eb_[:, m, :], st_=st_, sp_=sp_: e.matmul(
                                    smp[32 * qd:32 * qd + 32, :], lhsT=ones32[:, 0:32], rhs=src, start=st_, stop=sp_,
                                    tile_position=(0, 32 * qd)))
                            if back:
                                rds.append(("e", (u - back) % NE))
                        wrs.append("smp")
                    C.run("pe", fns, reads=rds, writes=wrs)
                    if last:
                        emit_epilogue(u, tcount)
                        tcount += 1
                        if i == nqt - 1:
                            load_hu(hi + 2)
                    for fn in deferred.pop(u, []):
                        fn()
                for k in sorted(deferred):
                    for fn in deferred[k]:
                        fn()
            C.barrier()

            with ExitStack() as st:
                Ob = [SB(st, "Ob%d" % i, [128, 2, 512], F32) for i in range(3)]
                Zb = [SB(st, "Zb%d" % i, [128, 2, 512], BF16) for i in range(3)]
                U = SB(st, "U", [128, 16, 512], BF16)
                U2 = SB(st, "U2", [128, 16, 512], BF16)
                Gt = [SB(st, "Gt%d" % i, [128, 2, 512], BF16) for i in range(2)]
                mg = SB(st, "mg", [128, 8, 512], BF16)
                xt3 = SB(st, "xt3", [128, 4, D], F32)
                sqr = [SB(st, "sq%d" % i, [128, 2, 512], F32) for i in range(3)]
                lnr = [SB(st, "ln%d" % i, [128, 2, 512], F32) for i in range(3)]
                ezr = [SB(st, "ez%d" % i, [128, 2, 512], F32) for i in range(3)]
                egr = [SB(st, "eg%d" % i, [128, 2, 512], F32) for i in range(2)]
                m1 = SB(st, "m1", [128, 512], F32)
                m2 = SB(st, "m2", [128, 512], F32)
                hres = [SB(st, "hres%d" % i, [128, D], F32) for i in range(2)]
                yst = [SB(st, "yst%d" % i, [128, D], F32) for i in range(2)]
                junk3 = SB(st, "junk3", [128, D], BF16)
                fss = SB(st, "fss", [128, 2], F32)
                fln = SB(st, "fln", [128, 2], F32)
                frs = SB(st, "frs", [128, 2], F32)
                pp = PS(st, "pp", [128, 2, 512], F32)
                yap = [PS(st, "yap%d" % i, [128, 512], F32) for i in range(2)]
                ybp = [PS(st, "ybp%d" % i, [128, 512], F32) for i in range(2)]
                opsb = [PS(st, "opsb%d" % i, [128, 512], F32) for i in range(2)]

                Ub = [U, U2]

                def tile_items(tt):
                    tok = tt * 512
                    Uc = Ub[tt % 2]
                    UK = lambda hd: ("U", tt % 2, hd)

                    def stage0(hp):
                        isa = hp < 4
                        o = hp % 3
                        for m in range(2):
                            hh = (2 * hp + m) % 8
                            C.dma("sp", Ob[o][:, m, :], (OaT if isa else ObT)[hh, :, tok:tok + 512], writes=[("Ob", o, m)])
                            C.dma("sp", Zb[o][:, m, :], (ZaT if isa else ZbT)[hh, :, tok:tok + 512], writes=[("Zb", o, m)])

                    def stage1(hp, r):
                        isa = hp < 4
                        o = hp % 3
                        OK_ = [("Ob", o, 0), ("Ob", o, 1)]
                        ZK_ = [("Zb", o, 0), ("Zb", o, 1)]
                        ez = ezr[r]
                        C.run("act", lambda e: e.activation(out=ez[:], in_=Zb[o][:], func=AF.Exp, scale=-1.0),
                              reads=ZK_, writes=[("ez", r)])
                        C.run("act", lambda e: e.activation(out=ez[:], in_=ez[:], func=AF.Ln, bias=onec[:], scale=1.0),
                              reads=[("ez", r)], writes=[("ez", r)])
                        C.run("act", lambda e: e.activation(out=ez[:], in_=ez[:], func=AF.Exp, scale=-1.0),
                              reads=[("ez", r)], writes=[("ez", r)])
                        C.run("pool", lambda e: e.tensor_tensor(out=ez[:], in0=Zb[o][:], in1=ez[:], op=ALU.mult),
                              reads=ZK_ + [("ez", r)], writes=[("ez", r)])
                        if isa:
                            C.run("pool", lambda e: e.tensor_tensor(out=sqr[r][:], in0=Ob[o][:], in1=Ob[o][:], op=ALU.mult),
                                  reads=OK_, writes=[("sq", r)])

                    def stage2(hp, r):
                        isa = hp < 4
                        o = hp % 3
                        OK_ = [("Ob", o, 0), ("Ob", o, 1)]
                        ez = ezr[r]
                        if isa:
                            sq = sqr[r]
                            ln_ = lnr[r]
                            C.run("pe", [lambda e: e.matmul(pp[:, 0, :], lhsT=onesf[:], rhs=sq[:, 0, :], start=True, stop=True),
                                         lambda e: e.matmul(pp[:, 1, :], lhsT=onesf[:], rhs=sq[:, 1, :], start=True, stop=True)],
                                  reads=[("sq", r)], writes=["pp"])
                            C.run("act", lambda e: e.activation(out=ln_[:], in_=pp[:], func=AF.Ln, bias=epsc[:], scale=1.0),
                                  reads=["pp"], writes=[("ln", r)])
                            C.run("act", lambda e: e.activation(out=ln_[:], in_=ln_[:], func=AF.Exp, scale=-0.5),
                                  reads=[("ln", r)], writes=[("ln", r)])
                            C.run("dve", lambda e: e.scalar_tensor_tensor(out=sq[:], in0=Ob[o][:], scalar=1.0, in1=ln_[:],
                                                                          op0=ALU.mult, op1=ALU.mult),
                                  reads=OK_ + [("ln", r)], writes=[("sq", r)])
                            C.run("dve", lambda e: e.scalar_tensor_tensor(out=Uc[:, 2 * hp:2 * hp + 2, :], in0=sq[:],
                                                                          scalar=sublnc[:, 0:1], in1=ez[:],
                                                                          op0=ALU.mult, op1=ALU.mult),
                                  reads=[("sq", r), ("ez", r)], writes=[UK(2 * hp), UK(2 * hp + 1)])
                        else:
                            C.run("dve", lambda e: e.scalar_tensor_tensor(out=Uc[:, 2 * hp:2 * hp + 2, :], in0=Ob[o][:],
                                                                          scalar=1.0, in1=ez[:], op0=ALU.mult, op1=ALU.mult),
                                  reads=OK_ + [("ez", r)], writes=[UK(2 * hp), UK(2 * hp + 1)])

                    def a_iter(hp):
                        if hp < 8:
                            stage0(hp)
                        if 1 <= hp <= 8:
                            stage1(hp - 1, (hp - 1) % 3)
                        if hp >= 2:
                            stage2(hp - 2, (hp - 2) % 3)

                    def b_item(c):
                        gi = c % 2
                        if c == 0:
                            C.dma("sp", xt3[:], xkv[tok:tok + 512, :].rearrange("(b p) c -> p b c", p=128), writes=["xt3"])
                        C.dma("sp", Gt[gi][:, 0, :], GaT[c, :, tok:tok + 512], writes=[("Gt", gi, 0)])
                        C.dma("sp", Gt[gi][:, 1, :], GbT[c, :, tok:tok + 512], writes=[("Gt", gi, 1)])
                        C.run("pe", [(lambda e, h=h: e.matmul(yap[gi][:], lhsT=Wpa[:, h, c * 128:(c + 1) * 128], rhs=Uc[:, h, :],
                                                               start=(h == 0), stop=(h == 7))) for h in range(8)],
                              reads=[("Wpa", 0), ("Wpa", 1)] + [UK(h) for h in range(8)], writes=[("yap", gi)])
                        C.run("pe", [(lambda e, h=h: e.matmul(ybp[gi][:], lhsT=Wpb[:, h, c * 128:(c + 1) * 128],
                                                               rhs=Uc[:, 8 + h, :], start=(h == 0), stop=(h == 7)))
                                     for h in range(8)],
                              reads=[("Wpb", 0), ("Wpb", 1)] + [UK(8 + h) for h in range(8)], writes=[("ybp", gi)])
                        eg = egr[gi]
                        C.run("act", lambda e: e.activation(out=eg[:], in_=Gt[gi][:], func=AF.Exp, scale=-1.0),
                              reads=[("Gt", gi, 0), ("Gt", gi, 1)], writes=[("eg", gi)])
                        C.run("act", lambda e: e.activation(out=eg[:], in_=eg[:], func=AF.Ln, bias=onec[:], scale=1.0),
                              reads=[("eg", gi)], writes=[("eg", gi)])
                        C.run("act", lambda e: e.activation(out=eg[:], in_=eg[:], func=AF.Exp, scale=-1.0),
                              reads=[("eg", gi)], writes=[("eg", gi)])
                        C.run("dve", lambda e: e.tensor_tensor(out=m1[:], in0=eg[:, 0, :], in1=yap[gi][:], op=ALU.mult),
                              reads=[("eg", gi), ("yap", gi)], writes=["m1"])
                        C.run("dve", lambda e: e.tensor_tensor(out=m2[:], in0=eg[:, 1, :], in1=ybp[gi][:], op=ALU.mult),
                              reads=[("eg", gi), ("ybp", gi)], writes=["m2"])
                        C.run("dve", lambda e: e.scalar_tensor_tensor(out=mg[:, c, :], in0=m1[:], scalar=1.0, in1=m2[:],
                                                                      op0=ALU.mult, op1=ALU.add),
                              reads=["m1", "m2"], writes=[("mg", c)])

                    def c_item(b):
                        hi_ = b % 2
                        for half in range(2):
                            oi = (2 * b + half) % 2
                            C.run("pe", [(lambda e, c=c: e.matmul(opsb[oi][:], lhsT=mg[:, c, b * 128:(b + 1) * 128],
                                                                   rhs=Wo[:, c, half * 512:(half + 1) * 512],
                                                                   start=(c == 0), stop=(c == 7))) for c in range(8)],
                                  reads=[("mg", c) for c in range(8)] + [("Wo", half)], writes=[("opsb", oi)])
                            C.run("dve", lambda e: e.tensor_tensor(out=hres[hi_][:, half * 512:(half + 1) * 512],
                                                                   in0=xt3[:, b, half * 512:(half + 1) * 512],
                                                                   in1=opsb[oi][:], op=ALU.add),
                                  reads=["xt3", ("opsb", oi)], writes=[("hres", hi_, half)])
                        C.run("act", lambda e: e.activation(out=junk3[:], in_=hres[hi_][:], func=AF.Square,
                                                            accum_out=fss[:, hi_:hi_ + 1]),
                              reads=[("hres", hi_, 0), ("hres", hi_, 1)], writes=["junk3", ("fss", hi_)])
                        C.run("act", lambda e: e.activation(out=fln[:, hi_:hi_ + 1], in_=fss[:, hi_:hi_ + 1], func=AF.Ln,
                                                            bias=epsc[:], scale=1.0 / D),
                              reads=[("fss", hi_)], writes=[("fln", hi_)])
                        C.run("act", lambda e: e.activation(out=frs[:, hi_:hi_ + 1], in_=fln[:, hi_:hi_ + 1], func=AF.Exp,
                                                            scale=-0.5), reads=[("fln", hi_)], writes=[("frs", hi_)])
                        C.run("dve", lambda e: e.scalar_tensor_tensor(out=yst[hi_][:], in0=hres[hi_][:],
                                                                      scalar=frs[:, hi_:hi_ + 1], in1=gf_bc[:],
                                                                      op0=ALU.mult, op1=ALU.mult),
                              reads=[("hres", hi_, 0), ("hres", hi_, 1), ("frs", hi_)], writes=[("yst", hi_)])
                        C.dma("pool", outp[tok + b * 128:tok + (b + 1) * 128, :], yst[hi_][:], reads=[("yst", hi_)])

                    return ([(lambda hp=hp: a_iter(hp)) for hp in range(10)],
                            [(lambda c=c: b_item(c)) for c in range(8)] + [(lambda b=b: c_item(b)) for b in range(4)])

                items = [tile_items(tt) for tt in range(nqt)]
                for f in items[0][0]:
                    f()
                for tt in range(nqt):
                    nxt = items[tt + 1][0] if tt + 1 < nqt else []
                    bc = items[tt][1]
                    for k, f in enumerate(bc):
                        f()
                        if k < len(nxt):
                            nxt[k]()
                    for f in nxt[len(bc):]:
                        f()
            C.barrier()
            jw.close()
        nc._n_emitted = C.n_ins
    return nc


def _rel_bucket_np(rel):
    half, max_exact = 16, 8
    ret = (rel > 0).astype(np.int32) * half
    n = np.abs(rel)
    nf = np.maximum(n, 1).astype(np.float32)
    large = max_exact + (np.log(nf / np.float32(max_exact)) / np.float32(math.log(128 / max_exact))
                         * np.float32(half - max_exact)).astype(np.int32)
    large = np.minimum(large, half - 1)
    return ret + np.where(n < max_exact, n, large)


def _rope_table(pos):
    pos = np.asarray(pos)
    row = (pos // 64).astype(np.float32)
    col = (pos % 64).astype(np.float32)
    inv = (np.float32(10000.0) ** (-np.arange(0, 64, 2, dtype=np.float32) / np.float32(64))).astype(np.float32)
    ar = (row[:, None] * inv[None, :]).astype(np.float32)
    ac = (col[:, None] * inv[None, :]).astype(np.float32)
    cr, sr, cc, sc = np.cos(ar), np.sin(ar), np.cos(ac), np.sin(ac)
    Ct = np.concatenate([cr, cr, cc, cc], axis=1)
    Sg = np.concatenate([-sr, sr, -sc, sc], axis=1)
    return np.ascontiguousarray(np.concatenate([Ct, Sg], axis=1).astype(np.float32))


def _onehots(r, nq):
    m = np.arange(1280)
    bk = _rel_bucket_np(m - 639)
    oh = np.zeros((32, 1280), np.float32)
    oh[bk[:1279], m[:1279]] = 1.0
    if r < nq - 1:
        ohn = np.ascontiguousarray(oh[:, 640:1280])
    else:
        ohn = np.zeros((32, 640), np.float32)
        ohn[15, :] = 1.0
    if r > 0:
        ohp = np.ascontiguousarray(oh[:, 0:640])
    else:
        ohp = np.zeros((32, 640), np.float32)
        ohp[31, :] = 1.0
    return oh, ohn, ohp


def _run(cfg, x_prompt, x_sample, g_norm, w_in, lambda_q1, lambda_k1, lambda_q2, lambda_k2, subln_w,
         q_norm_b, k_norm_b, w_proj_a, w_proj_b, w_out, rel_bias, g_final, n_cores=8):
    NP, SP, SSK, SSQ = cfg["NP"], cfg["SP"], cfg["SSK"], cfg["SSQ"]
    nq = SSK // SSQ
    f = lambda a: np.ascontiguousarray(np.asarray(a, dtype=np.float32))
    x_prompt, x_sample = f(x_prompt), f(x_sample)
    nc = _build(cfg)
    common = {
        "g_norm": f(g_norm).reshape(1, D), "g_final": f(g_final).reshape(1, D), "w_in": f(w_in).reshape(D, INW),
        "lamq": np.ascontiguousarray(np.stack([f(lambda_q1).reshape(64), f(lambda_k1).reshape(64),
                                               f(lambda_q2).reshape(64), f(lambda_k2).reshape(64)])),
        "subln_w": f(subln_w).reshape(1, 128), "q_norm_b": f(q_norm_b).reshape(1, 128),
        "k_norm_b": f(k_norm_b).reshape(1, 128), "w_pa": f(w_proj_a).reshape(D, D), "w_pb": f(w_proj_b).reshape(D, D),
        "w_o": f(w_out).reshape(D, D), "rel_bias": f(rel_bias).reshape(32, 8),
        "ropeP": _rope_table(np.arange(SP)), "eye": np.eye(128, dtype=np.float32),
    }
    nown = SSQ // 128
    nkb = SSK // 128
    in_maps = []
    for c in range(n_cores):
        r = c % nq
        sidx = c // nq
        pos = (np.arange(SSK) + r * SSQ) % SSK
        oh, ohn, ohp = _onehots(r, nq)
        jl = np.arange(nown, nkb)
        side = ((r * nown + jl) < nkb).astype(np.float32).reshape(1, -1)
        m = dict(common)
        m["xp"] = np.ascontiguousarray(x_prompt[NP * c:NP * (c + 1)])
        m["xs"] = np.ascontiguousarray(x_sample[sidx][pos])
        m["ropeS"] = _rope_table(pos)
        m["onehot"] = oh
        m["onehot_n"] = ohn
        m["onehot_p"] = ohp
        m["side"] = np.ascontiguousarray(side)
        in_maps.append(m)
    res = run_bass_kernel_spmd(nc, in_maps, core_ids=list(range(n_cores)))
    yp = np.concatenate([res.results[c]["yp"] for c in range(n_cores)], axis=0)
    ys = np.zeros_like(x_sample)
    for c in range(n_cores):
        r = c % nq
        ys[c // nq, r * SSQ:(r + 1) * SSQ] = res.results[c]["ys"]
    return yp.astype(np.float32), ys.astype(np.float32)


def kernel(x_prompt, x_sample, g_norm, w_in, lambda_q1, lambda_k1, lambda_q2, lambda_k2, subln_w,
           q_norm_b, k_norm_b, w_proj_a, w_proj_b, w_out, rel_bias, g_final):
    return _run(FULL_CFG, x_prompt, x_sample, g_norm, w_in, lambda_q1, lambda_k1, lambda_q2, lambda_k2, subln_w,
                q_norm_b, k_norm_b, w_proj_a, w_proj_b, w_out, rel_bias, g_final)
```
